# Optimizing a Trainium2 kernel written in Bass

```python
import jax, jax.numpy as jnp
from jax import lax
import numpy as np

D_MODEL = 1024
BATCH = 8
SEQ = 2048
DEPTH = 2
DEC_BATCH = 128
DEC_SEQ = 4
PAST_LEN = 16384
PAGE_SIZE = 128

EPS = 1e-6
FFN_DIM = 2816
GLA_HEADS = 4
GLA_DK_TOTAL = D_MODEL // 2
GLA_DV_TOTAL = D_MODEL
GLA_DK = GLA_DK_TOTAL // GLA_HEADS
GLA_DV = GLA_DV_TOTAL // GLA_HEADS
GLA_GATE_RANK = 16
GLA_TAU = 16.0
GLA_CHUNK = 64
SSM_INNER = 2 * D_MODEL
SSM_HEADDIM = 64
SSM_HEADS = SSM_INNER // SSM_HEADDIM
SSM_GROUPS = 4
SSM_DSTATE = 128
SSM_CHUNK = 64
CONV_W = 4
CONV_DIM = SSM_INNER + 2 * SSM_GROUPS * SSM_DSTATE
IN_SPLITS = (GLA_DK_TOTAL, GLA_DK_TOTAL, GLA_DV_TOTAL, GLA_DV_TOTAL, GLA_GATE_RANK,
             SSM_INNER, CONV_DIM, SSM_HEADS, D_MODEL, D_MODEL)
IN_DIM = sum(IN_SPLITS)

kernel_name = "hybrid_gla_ssd_macaron_step"


def rmsnorm(x, w):
    xf = x.astype(jnp.float32)
    y = xf * lax.rsqrt(jnp.mean(xf * xf, axis=-1, keepdims=True) + EPS)
    return (y * w.astype(jnp.float32)).astype(x.dtype)


def swiglu(x, w_gu, w_down):
    g, u = jnp.split(x @ w_gu, 2, axis=-1)
    return (jax.nn.silu(g) * u) @ w_down


def gla_chunked(q, k, v, log_a, s0):
    b, L, H, _ = q.shape
    DV = v.shape[-1]
    C = min(GLA_CHUNK, L)
    n = -(-L // C)
    pad = n * C - L

    def chunks(t):
        t = jnp.pad(t.astype(jnp.float32), ((0, 0), (0, pad), (0, 0), (0, 0)))
        return t.reshape(b, n, C, H, t.shape[-1]).swapaxes(0, 1)

    qc = chunks(q) * (GLA_DK ** -0.5)
    kc, vc, ac = chunks(k), chunks(v), chunks(log_a)
    bcum = jnp.cumsum(ac, axis=2)
    blast = bcum[:, :, -1:]
    q_in = qc * jnp.exp(bcum)
    k_in = kc * jnp.exp(-bcum)
    k_out = kc * jnp.exp(blast - bcum)
    causal = jnp.tril(jnp.ones((C, C), dtype=bool))
    scores = jnp.where(causal, jnp.einsum('nbthd,nbshd->nbhts', q_in, k_in), 0.0)
    o_intra = jnp.einsum('nbhts,nbshv->nbthv', scores, vc)
    dS = jnp.einsum('nbshd,nbshv->nbhdv', k_out, vc)
    decay = jnp.exp(blast[:, :, 0])

    def step(S, inp):
        dS_c, dec_c = inp
        return dec_c[..., None] * S + dS_c, S

    S_fin, S_prev = lax.scan(step, s0.astype(jnp.float32), (dS, decay))
    o_inter = jnp.einsum('nbthd,nbhdv->nbthv', q_in, S_prev)
    o = (o_intra + o_inter).swapaxes(0, 1).reshape(b, n * C, H, DV)[:, :L]
    return o, S_fin


def ssd_chunked(x, dt, A, Bm, Cm, h0):
    b, L, H, P = x.shape
    G, N = Bm.shape[2], Bm.shape[3]
    hpg = H // G
    C = min(SSM_CHUNK, L)
    n = -(-L // C)
    pad = n * C - L

    def chunks(t):
        t = jnp.pad(t.astype(jnp.float32), ((0, 0), (0, pad)) + ((0, 0),) * (t.ndim - 2))
        return t.reshape(b, n, C, *t.shape[2:]).swapaxes(0, 1)

    xg = chunks(x.astype(jnp.float32) * dt[..., None]).reshape(n, b, C, G, hpg, P)
    acum = jnp.cumsum(chunks(dt * A).reshape(n, b, C, G, hpg), axis=2)
    Bc, Cc = chunks(Bm), chunks(Cm)
    causal = jnp.tril(jnp.ones((C, C), dtype=bool))[:, :, None, None]
    seg = jnp.exp(jnp.where(causal, acum[:, :, :, None] - acum[:, :, None], -jnp.inf))
    cb = jnp.einsum('nbtgd,nbsgd->nbtsg', Cc, Bc)
    y_diag = jnp.einsum('nbtsg,nbtsgh,nbsghp->nbtghp', cb, seg, xg)
    dS = jnp.einsum('nbsgd,nbsgh,nbsghp->nbghpd', Bc, jnp.exp(acum[:, :, -1:] - acum), xg)
    decay = jnp.exp(acum[:, :, -1])

    def step(h, inp):
        dS_c, dec_c = inp
        return dec_c[..., None, None] * h + dS_c, h

    h_fin, h_prev = lax.scan(step, h0.astype(jnp.float32).reshape(b, G, hpg, P, N), (dS, decay))
    y_off = jnp.einsum('nbtgd,nbtgh,nbghpd->nbtghp', Cc, jnp.exp(acum), h_prev)
    y = (y_diag + y_off).swapaxes(0, 1).reshape(b, n * C, H, P)[:, :L]
    return y, h_fin.reshape(b, H, P, N)


def causal_dwconv(xbc, conv_state, w, bias):
    L = xbc.shape[1]
    full = jnp.concatenate([conv_state.astype(xbc.dtype), xbc], axis=1)
    y = bias + sum(full[:, i:i + L] * w[i] for i in range(CONV_W))
    return y, full[:, -(CONV_W - 1):]


def mixer(u, gla0, ssm0, conv0, w_in, w_gla_a2, b_gla_a2, gla_norm, conv_w, conv_b,
          dt_bias, a_log, d_skip, ssm_norm, w_proj_gla, w_proj_ssm, w_out):
    b, L, _ = u.shape
    offs = np.cumsum(IN_SPLITS)[:-1].tolist()
    q, k, v, g, a_lr, z, xbc, dt_raw, gate_a, gate_b = jnp.split(u @ w_in, offs, axis=-1)
    log_a = jax.nn.log_sigmoid((a_lr @ w_gla_a2 + b_gla_a2).astype(jnp.float32)) / GLA_TAU
    o_gla, gla_new = gla_chunked(q.reshape(b, L, GLA_HEADS, GLA_DK),
                                 k.reshape(b, L, GLA_HEADS, GLA_DK),
                                 v.reshape(b, L, GLA_HEADS, GLA_DV),
                                 log_a.reshape(b, L, GLA_HEADS, GLA_DK), gla0)
    o_gla = rmsnorm(o_gla, gla_norm.reshape(GLA_HEADS, GLA_DV)).reshape(b, L, GLA_DV_TOTAL)
    o_gla = (o_gla * jax.nn.silu(g.astype(jnp.float32))).astype(u.dtype)
    xbc_c, conv_new = causal_dwconv(xbc, conv0, conv_w, conv_b)
    xbc_c = jax.nn.silu(xbc_c)
    xs, Bm, Cm = jnp.split(xbc_c, [SSM_INNER, SSM_INNER + SSM_GROUPS * SSM_DSTATE], axis=-1)
    xs = xs.reshape(b, L, SSM_HEADS, SSM_HEADDIM)
    dt = jax.nn.softplus((dt_raw + dt_bias).astype(jnp.float32))
    A = -jnp.exp(a_log.astype(jnp.float32))
    y, ssm_new = ssd_chunked(xs, dt, A,
                             Bm.reshape(b, L, SSM_GROUPS, SSM_DSTATE),
                             Cm.reshape(b, L, SSM_GROUPS, SSM_DSTATE), ssm0)
    y = (y + d_skip.astype(jnp.float32)[:, None] * xs.astype(jnp.float32)).reshape(b, L, SSM_INNER)
    y = rmsnorm(y * jax.nn.silu(z.astype(jnp.float32)), ssm_norm).astype(u.dtype)
    m = jax.nn.sigmoid(gate_a) * (o_gla @ w_proj_gla) + jax.nn.sigmoid(gate_b) * (y @ w_proj_ssm)
    return m @ w_out, gla_new, ssm_new, conv_new


def trunk(x, gla0, ssm0, conv0, params):
    (norm_ffn1, w_ffn1_gu, w_ffn1_down, norm_mix, w_in, w_gla_a2, b_gla_a2, gla_norm,
     conv_w, conv_b, dt_bias, a_log, d_skip, ssm_norm, w_proj_gla, w_proj_ssm, w_out,
     norm_ffn2, w_ffn2_gu, w_ffn2_down, norm_final) = params
    glas, ssms, convs = [], [], []
    for l in range(DEPTH):
        x = x + 0.5 * swiglu(rmsnorm(x, norm_ffn1[l]), w_ffn1_gu[l], w_ffn1_down[l])
        m, g_s, s_s, c_s = mixer(rmsnorm(x, norm_mix[l]), gla0[l], ssm0[l], conv0[l],
                                 w_in[l], w_gla_a2[l], b_gla_a2[l], gla_norm[l], conv_w[l], conv_b[l],
                                 dt_bias[l], a_log[l], d_skip[l], ssm_norm[l],
                                 w_proj_gla[l], w_proj_ssm[l], w_out[l])
        x = x + m
        x = x + 0.5 * swiglu(rmsnorm(x, norm_ffn2[l]), w_ffn2_gu[l], w_ffn2_down[l])
        glas.append(g_s)
        ssms.append(s_s)
        convs.append(c_s)
    return rmsnorm(x, norm_final), jnp.stack(glas), jnp.stack(ssms), jnp.stack(convs)


def setup_inputs(seed: int = 0) -> dict:
    key = jax.random.key(seed)
    ks = iter(jax.random.split(key, 32))

    def nrm(shape, scale):
        return jax.random.normal(next(ks), shape, jnp.float32) * scale

    def gain(shape):
        return 1.0 + nrm(shape, 0.02)

    Ld = DEPTH
    dt0 = jnp.exp(jax.random.uniform(next(ks), (Ld, SSM_HEADS), jnp.float32,
                                     np.log(1e-3), np.log(1e-1)))
    dt_bias = dt0 + jnp.log(-jnp.expm1(-dt0))
    a_log = jnp.log(jax.random.uniform(next(ks), (Ld, SSM_HEADS), jnp.float32, 1.0, 16.0))
    return {
        "x_prompt": nrm((BATCH, SEQ, D_MODEL), 1.0),
        "x_sample": nrm((DEC_BATCH, DEC_SEQ, D_MODEL), 1.0),
        "state_gla": nrm((Ld, DEC_BATCH, GLA_HEADS, GLA_DK, GLA_DV), 1.0),
        "state_ssm": nrm((Ld, DEC_BATCH, SSM_HEADS, SSM_HEADDIM, SSM_DSTATE), 0.1),
        "state_conv": nrm((Ld, DEC_BATCH, CONV_W - 1, CONV_DIM), 1.0),
        "norm_ffn1": gain((Ld, D_MODEL)),
        "w_ffn1_gu": nrm((Ld, D_MODEL, 2 * FFN_DIM), D_MODEL ** -0.5),
        "w_ffn1_down": nrm((Ld, FFN_DIM, D_MODEL), FFN_DIM ** -0.5),
        "norm_mix": gain((Ld, D_MODEL)),
        "w_in": nrm((Ld, D_MODEL, IN_DIM), D_MODEL ** -0.5),
        "w_gla_a2": nrm((Ld, GLA_GATE_RANK, GLA_DK_TOTAL), GLA_GATE_RANK ** -0.5),
        "b_gla_a2": nrm((Ld, GLA_DK_TOTAL), 0.1),
        "gla_norm": gain((Ld, GLA_DV_TOTAL)),
        "conv_w": nrm((Ld, CONV_W, CONV_DIM), CONV_W ** -0.5),
        "conv_b": nrm((Ld, CONV_DIM), 0.02),
        "dt_bias": dt_bias,
        "a_log": a_log,
        "d_skip": gain((Ld, SSM_HEADS)),
        "ssm_norm": gain((Ld, SSM_INNER)),
        "w_proj_gla": nrm((Ld, GLA_DV_TOTAL, D_MODEL), GLA_DV_TOTAL ** -0.5),
        "w_proj_ssm": nrm((Ld, SSM_INNER, D_MODEL), SSM_INNER ** -0.5),
        "w_out": nrm((Ld, D_MODEL, D_MODEL), D_MODEL ** -0.5),
        "norm_ffn2": gain((Ld, D_MODEL)),
        "w_ffn2_gu": nrm((Ld, D_MODEL, 2 * FFN_DIM), D_MODEL ** -0.5),
        "w_ffn2_down": nrm((Ld, FFN_DIM, D_MODEL), FFN_DIM ** -0.5),
        "norm_final": gain((D_MODEL,)),
    }


def reference(x_prompt, x_sample, state_gla, state_ssm, state_conv,
              norm_ffn1, w_ffn1_gu, w_ffn1_down, norm_mix, w_in, w_gla_a2, b_gla_a2, gla_norm,
              conv_w, conv_b, dt_bias, a_log, d_skip, ssm_norm, w_proj_gla, w_proj_ssm, w_out,
              norm_ffn2, w_ffn2_gu, w_ffn2_down, norm_final):
    params = (norm_ffn1, w_ffn1_gu, w_ffn1_down, norm_mix, w_in, w_gla_a2, b_gla_a2, gla_norm,
              conv_w, conv_b, dt_bias, a_log, d_skip, ssm_norm, w_proj_gla, w_proj_ssm, w_out,
              norm_ffn2, w_ffn2_gu, w_ffn2_down, norm_final)
    bp = x_prompt.shape[0]
    gla0 = jnp.zeros((DEPTH, bp, GLA_HEADS, GLA_DK, GLA_DV), jnp.float32)
    ssm0 = jnp.zeros((DEPTH, bp, SSM_HEADS, SSM_HEADDIM, SSM_DSTATE), jnp.float32)
    conv0 = jnp.zeros((DEPTH, bp, CONV_W - 1, CONV_DIM), x_prompt.dtype)
    y_prompt, gla_p, ssm_p, conv_p = trunk(x_prompt, gla0, ssm0, conv0, params)
    y_sample, gla_s, ssm_s, conv_s = trunk(x_sample, state_gla, state_ssm, state_conv, params)
    return (y_prompt, y_sample, gla_p, ssm_p, conv_p, gla_s, ssm_s, conv_s)
```

```python
import contextlib
import numpy as np
import concourse.bass as bass
import concourse.mybir as mybir
from concourse.bass_utils import run_bass_kernel_spmd

F32 = mybir.dt.float32
BF16 = mybir.dt.bfloat16
AF = mybir.ActivationFunctionType
ALU = mybir.AluOpType
AX = mybir.AxisListType

_DS = {F32: 4, BF16: 2}
_MAX_OPS = 10 ** 9
_DEBUG = False
_SKIP = ""
_NOWDMA = False
ENGS = ["pe", "act", "dve", "pool", "sp"]
DMAQ = {"sp": 6, "pool": 10, "act": 2}

D = 1024
FF = 2816
NL = 2
EPS = 1e-6
IN_DIM = 10288
O_Q, O_K, O_V, O_G, O_A, O_Z, O_X, O_DT, O_GA, O_GB = 0, 512, 1024, 2048, 3072, 3088, 5136, 8208, 8240, 9264
NPASS = 4
TP = 512
TS = 64
T = TP + TS


class Prog:
    def __init__(self, nc):
        self.nc = nc
        self.ops = {e: [] for e in ENGS}
        self.recs = {"sb": [], "ps": []}
        self.seen = {e: {} for e in ENGS}
        self.slot_cnt = {(q, i): 0 for q in DMAQ for i in range(DMAQ[q])}
        self.slot_next = {q: 0 for q in DMAQ}
        self.base = {}
        self.sb_off = 16512
        self.nps = 0
        self.uid = 0
        self.nops = 0
        self.max_ops = _MAX_OPS

    def sb(self, name, shape, dtype, at=None):
        row = int(np.prod(shape[1:])) * _DS[dtype]
        if at is None:
            at = self.sb_off
            self.sb_off = (at + row + 63) // 64 * 64
        assert at + row <= 229376, (name, at, row)
        self.uid += 1
        t = self.nc.alloc_sbuf_tensor_at("%s_u%d" % (name, self.uid), list(shape), dtype, offset=at)
        self.base[t.name] = ("sb", at, row)
        return t

    def ps(self, name, shape, dtype):
        t = self.nc.alloc_psum_tensor(name, list(shape), dtype)
        row = int(np.prod(shape[1:])) * _DS[dtype]
        self.base[t.name] = ("ps", self.nps * 2048, row)
        self.nps += (row + 2047) // 2048
        return t

    def box(self, ap):
        name = ap.tensor.name
        if name not in self.base:
            return None
        space, b0, row = self.base[name]
        ds = _DS[ap.dtype]
        rowe = row // ds
        off = ap.offset
        pats = ap.ap
        p_lo = off // rowe
        f_lo = (off % rowe) * ds
        pstep, pcnt = pats[0]
        p_hi = p_lo + (pstep // rowe) * (pcnt - 1) + 1
        ext = sum(abs(s) * (c - 1) for s, c in pats[1:]) + 1
        f_hi = f_lo + ext * ds
        return (space, p_lo, p_hi, b0 + f_lo, b0 + f_hi)

    def _deps(self, eng, reads, writes, ev, skip_same):
        deps = set()
        rb = [b for b in (self.box(a) for a in reads) if b is not None]
        wb = [b for b in (self.box(a) for a in writes) if b is not None]
        if eng == "pe":
            wb = [(b[0], 0, 128, b[3] // 2048 * 2048, (b[4] + 2047) // 2048 * 2048) if b[0] == "ps" else b
                  for b in wb]
        for space in ("sb", "ps"):
            rbs = [b for b in rb if b[0] == space]
            wbs = [b for b in wb if b[0] == space]
            if not rbs and not wbs:
                continue
            keep = []
            for rec in self.recs[space]:
                (_, pl, ph, fl, fh), kind, rev, reng = rec
                hit = False
                covered = False
                for b in wbs:
                    if pl < b[2] and b[1] < ph and fl < b[4] and b[3] < fh:
                        hit = True
                        if b[1] <= pl and ph <= b[2] and b[3] <= fl and fh <= b[4]:
                            covered = True
                if not hit and kind == "w":
                    for b in rbs:
                        if pl < b[2] and b[1] < ph and fl < b[4] and b[3] < fh:
                            hit = True
                            break
                if hit:
                    deps.add(rev)
                if covered:
                    continue
                if kind == "r" and reng == eng and rev[0] == "eng" and ev[0] == "eng":
                    sup = False
                    for b in rbs:
                        if b[1] <= pl and ph <= b[2] and b[3] <= fl and fh <= b[4]:
                            sup = True
                            break
                    if sup:
                        continue
                keep.append(rec)
            for b in rbs:
                keep.append((b, "r", ev, eng))
            for b in wbs:
                keep.append((b, "w", ev, eng))
            self.recs[space] = keep
        seen = self.seen[eng]
        best = {}
        for d in deps:
            if d[0] == "eng":
                _, x, i = d
                if x == eng and (x == "pe" or skip_same):
                    continue
                if seen.get(x, -1) >= i:
                    continue
            else:
                if seen.get(d[1], 0) >= d[2]:
                    continue
            k = d[1]
            if k not in best or best[k][2] < d[2]:
                best[k] = d
        out = []
        for k, d in best.items():
            seen[k] = d[2]
            if d[0] == "eng":
                self.ops[d[1]][d[2]]["needed"] = True
            out.append(d)
        return out

    def op(self, eng, fn, reads=(), writes=()):
        self.nops += 1
        if self.nops > self.max_ops:
            return None
        idx = len(self.ops[eng])
        ev = ("eng", eng, idx)
        waits = self._deps(eng, reads, writes, ev, False)
        self.ops[eng].append({"fn": fn, "waits": waits, "needed": False, "dma": None})
        return ev

    def dma(self, q, out, in_, **kw):
        self.nops += 1
        if self.nops > self.max_ops:
            return None
        k = self.slot_next[q]
        self.slot_next[q] = (k + 1) % DMAQ[q]
        slot = (q, k)
        prev = self.slot_cnt[slot]
        self.slot_cnt[slot] = prev + 1
        ev = ("dma", slot, prev + 1)
        waits = self._deps(q, [in_], [out], ev, False)
        if prev > 0 and self.seen[q].get(slot, 0) < prev:
            waits.append(("dma", slot, prev))
            self.seen[q][slot] = prev
        self.ops[q].append({
            "fn": lambda e, o=out, i=in_, kw=kw: e.dma_start(out=o, in_=i, **kw),
            "waits": waits, "needed": False, "dma": slot})
        return ev

    def mm(self, out, lhsT, rhs, start=True, stop=True):
        return self.op("pe", lambda e: e.matmul(out, lhsT, rhs, start=start, stop=stop),
                       reads=[lhsT, rhs], writes=[out])

    def tr(self, out, in_, ident):
        return self.op("pe", lambda e: e.transpose(out, in_, ident),
                       reads=[in_, ident], writes=[out])

    def act(self, out, in_, func, bias=None, scale=None, accum_out=None):
        kw = {}
        reads = [in_]
        writes = [out]
        if bias is not None:
            kw["bias"] = bias
            if not isinstance(bias, (int, float)):
                reads.append(bias)
        if scale is not None:
            kw["scale"] = scale
            if not isinstance(scale, (int, float)):
                reads.append(scale)
        if accum_out is not None:
            kw["accum_out"] = accum_out
            writes.append(accum_out)
        return self.op("act", lambda e: e.activation(out, in_, func, **kw),
                       reads=reads, writes=writes)

    def tt(self, out, in0, in1, op, eng="dve"):
        return self.op(eng, lambda e: e.tensor_tensor(out, in0, in1, op),
                       reads=[in0, in1], writes=[out])

    def ts(self, out, in0, s1, op0, s2=None, op1=None, eng="dve"):
        reads = [in0] + [s for s in (s1, s2) if s is not None and not isinstance(s, (int, float))]
        if op1 is None:
            f = lambda e: e.tensor_scalar(out, in0, s1, None, op0)
        else:
            f = lambda e: e.tensor_scalar(out, in0, s1, s2, op0, op1)
        return self.op(eng, f, reads=reads, writes=[out])

    def stt(self, out, in0, scalar, in1, op0, op1, eng="dve"):
        reads = [in0, in1] + ([scalar] if not isinstance(scalar, (int, float)) else [])
        return self.op(eng, lambda e: e.scalar_tensor_tensor(out, in0, scalar, in1, op0, op1),
                       reads=reads, writes=[out])

    def copy(self, out, in_, eng="dve"):
        if eng == "act":
            return self.op("act", lambda e: e.activation(out, in_, AF.Copy), reads=[in_], writes=[out])
        return self.op(eng, lambda e: e.tensor_copy(out, in_), reads=[in_], writes=[out])

    def memset(self, ap, val, eng="dve"):
        return self.op(eng, lambda e: e.memset(ap, val), writes=[ap])

    def reduce_sum(self, out, in_, eng="dve"):
        return self.op(eng, lambda e: e.tensor_reduce(out, in_, AX.X, ALU.add), reads=[in_], writes=[out])

    def emit(self):
        nc = self.nc
        with contextlib.ExitStack() as st:
            esem = {e: st.enter_context(nc.semaphore("s_" + e)) for e in ENGS if e != "sp"}
            dsem = {s: st.enter_context(nc.semaphore("d_%s%d" % s)) for s in self.slot_cnt}
            val = {}
            for e in ENGS:
                c = 0
                for i, o in enumerate(self.ops[e]):
                    if o["needed"]:
                        c += 1
                        val[(e, i)] = c
            block = st.enter_context(nc.Block())

            def run(ename, eng):
                for o in self.ops[ename]:
                    for w in o["waits"]:
                        if w[0] == "eng":
                            eng.wait_ge(esem[w[1]], val[(w[1], w[2])])
                        else:
                            eng.wait_ge(dsem[w[1]], 16 * w[2])
                    ins = o["fn"](eng)
                    if o["needed"]:
                        ins.then_inc(esem[ename], 1)
                    if o["dma"] is not None:
                        ins.then_inc(dsem[o["dma"]], 16)
                if ename in DMAQ:
                    for i in range(DMAQ[ename]):
                        c = self.slot_cnt[(ename, i)]
                        if c > 0:
                            eng.wait_ge(dsem[(ename, i)], 16 * c)

            @block.tensor
            def _(e):
                run("pe", e)

            @block.scalar
            def _(e):
                run("act", e)

            @block.vector
            def _(e):
                run("dve", e)

            @block.gpsimd
            def _(e):
                run("pool", e)

            @block.sync
            def _(e):
                run("sp", e)


C_ID, C_CAUS, C_NCAUS, C_NSUF, C_ALLM1 = 0, 128, 256, 384, 512
C_CAUSS, C_NCAUSS, C_NSUFS, C_NSEQ, C_BMCOL, C_ONES, C_BM = 640, 704, 768, 832, 848, 864, 992
NCONST = 2016
NCONST_SB = 992


def make_consts():
    c = np.zeros((128, NCONST), np.float32)
    i = np.arange(128)
    c[:, C_ID:C_ID + 128] = np.eye(128)
    caus = (i[:, None] <= i[None, :]).astype(np.float32)
    c[:, C_CAUS:C_CAUS + 128] = caus
    c[:, C_NCAUS:C_NCAUS + 128] = -caus
    c[:, C_NSUF:C_NSUF + 128] = -(i[:, None] > i[None, :]).astype(np.float32)
    c[:, C_ALLM1:C_ALLM1 + 128] = -1.0
    j = np.arange(64)
    same = (j[:, None] // 4 == j[None, :] // 4)
    cs = ((j[:, None] <= j[None, :]) & same).astype(np.float32)
    c[:64, C_CAUSS:C_CAUSS + 64] = cs
    c[:64, C_NCAUSS:C_NCAUSS + 64] = -cs
    c[:64, C_NSUFS:C_NSUFS + 64] = -((j[:, None] > j[None, :]) & same).astype(np.float32)
    bm = (j[:, None] // 4 == np.arange(16)[None, :]).astype(np.float32)
    c[:64, C_NSEQ:C_NSEQ + 16] = -bm
    c[:64, C_BMCOL:C_BMCOL + 16] = bm
    c[:, C_BM:C_BM + 1024] = np.broadcast_to(bm.T.reshape(1, 1024), (128, 1024))
    c[:, C_ONES:C_ONES + 128] = 1.0
    return c


P_NF1, P_NMIX, P_NF2, P_GLAN, P_SSMN, P_CW, P_CB = 0, 16, 32, 48, 64, 96, 288
P_DTB, P_ALOG, P_DSK, P_W2, P_BGLA = 336, 400, 464, 528, 1552
NPAR = 2576


def make_params(inp):
    p = np.zeros((128, NPAR), np.float32)

    def col(w, nch):
        return np.ascontiguousarray(w.reshape(NL, nch, 128).transpose(2, 0, 1)).reshape(128, NL * nch)

    p[:, P_NF1:P_NF1 + 16] = col(inp["norm_ffn1"], 8)
    p[:, P_NMIX:P_NMIX + 16] = col(inp["norm_mix"], 8)
    p[:, P_NF2:P_NF2 + 16] = col(inp["norm_ffn2"], 8)
    p[:, P_GLAN:P_GLAN + 16] = col(inp["gla_norm"], 8)
    p[:, P_SSMN:P_SSMN + 32] = col(inp["ssm_norm"], 16)
    cw = inp["conv_w"].reshape(NL, 4, 24, 128).transpose(3, 0, 2, 1)
    p[:, P_CW:P_CW + 192] = np.ascontiguousarray(cw).reshape(128, 192)
    p[:, P_CB:P_CB + 48] = col(inp["conv_b"], 24)
    p[:, P_DTB:P_DTB + 64] = np.broadcast_to(inp["dt_bias"].reshape(1, 64), (128, 64))
    p[:, P_ALOG:P_ALOG + 64] = np.broadcast_to(inp["a_log"].reshape(1, 64), (128, 64))
    p[:, P_DSK:P_DSK + 64] = np.broadcast_to(inp["d_skip"].reshape(1, 64), (128, 64))
    p[:16, P_W2:P_W2 + 1024] = inp["w_gla_a2"].transpose(1, 0, 2).reshape(16, 1024)
    p[:, P_BGLA:P_BGLA + 1024] = np.broadcast_to(inp["b_gla_a2"].reshape(1, 1024), (128, 1024))
    return p


def build(last_pass=NPASS, do_mixer=True, stage_limit=3):
    nc = bass.Bass("TRN2", target_bir_lowering=False)
    P = Prog(nc)

    def din(name, shape):
        return nc.dram_tensor(name, list(shape), F32, kind="ExternalInput")

    def dout(name, shape):
        return nc.dram_tensor(name, list(shape), F32, kind="ExternalOutput")

    xp = din("xp", [2048, D])
    xs_in = din("xs", [TS, D])
    sg_in = din("sg", [NL, 16, 4, 128, 256])
    ss_in = din("ss", [NL, 16, 32, 64, 128])
    sc_in = din("sc", [NL, 16, 3, 3072])
    consts_in = din("consts", [128, NCONST])
    params_in = din("params", [128, NPAR])
    nfin_in = din("nfin", [128, D])
    bm_in = din("bm", [128, 1024])
    w_gu = [din("w_ffn1_gu", [NL, D, 2 * FF]), din("w_ffn2_gu", [NL, D, 2 * FF])]
    w_dn = [din("w_ffn1_down", [NL, FF, D]), din("w_ffn2_down", [NL, FF, D])]
    w_in = din("w_in", [NL, D, IN_DIM])
    w_pg = din("w_proj_gla", [NL, D, D])
    w_ps = din("w_proj_ssm", [NL, 2 * D, D])
    w_o = din("w_out", [NL, D, D])

    yp = dout("yp", [2048, D])
    ys = dout("ys", [TS, D])
    gp = dout("gp", [NL, 4, 128, 256])
    spo = dout("spo", [NL, 32, 64, 128])
    cpo = dout("cpo", [NL, 3, 3072])
    gso = dout("gso", [NL, 16, 4, 128, 256])
    sso = dout("sso", [NL, 16, 32, 64, 128])
    cso = dout("cso", [NL, 16, 3, 3072])
    dbg = dout("dbg", [128, 8, 512]) if _DEBUG else None
    dbg_done = set()

    def dump(i, ap, np_=128, nf=512):
        if dbg is None or i in dbg_done:
            return
        dbg_done.add(i)
        P.dma("sp", dbg[:np_, i, :nf], ap)

    CON = P.sb("CON", [128, NCONST_SB], F32)
    BMt = P.sb("BMt", [128, 16, 64], BF16)
    PAR = P.sb("PAR", [128, NPAR], F32)
    identb = P.sb("identb", [128, 128], BF16)
    x = P.sb("x", [128, 5, D], F32)
    xnT = P.sb("xnT", [128, 8, T], BF16)
    NSLOT = 3
    SLOTE = 4096
    wring = P.sb("wring", [128, NSLOT, SLOTE], BF16)
    Sst = P.sb("Sst", [128, NL, 1024], F32)
    Sbf = P.sb("Sbf", [128, 1024], BF16)
    hT = P.sb("hT", [128, NL, 2048], F32)
    hTbf = P.sb("hTbf", [128, 2048], BF16)
    halo = P.sb("halo", [128, NL, 24, 3], BF16)
    xsb = P.sb("xsb", [128, D], BF16)
    junk = P.sb("junk", [128, D], BF16)
    st1 = P.sb("st1", [128, 8], F32)
    st2 = P.sb("st2", [128, 8], F32)
    expA = P.sb("expA", [128, 64], F32)
    onesb = P.sb("onesb", [128, 128], BF16)
    hl = P.sb("hl", [128, 4], BF16)
    hlf = P.sb("hlf", [128, 4], F32)
    rrow = P.sb("rrow", [128, 256], BF16)
    A0 = P.sb_off

    def cv(off, n, rows=128):
        return CON[:rows, off:off + n]

    identf = cv(C_ID, 128)
    ones_row = CON[0:1, C_ONES:C_ONES + 128]

    def par(off, n, rows=128):
        return PAR[:rows, off:off + n]

    PP = [P.ps("pp%d" % i, [128, 1024], F32) for i in range(4)]
    bstate = {"b": 0}
    reserved = set()

    def bank():
        while True:
            k = bstate["b"]
            bstate["b"] = (k + 1) % 8
            if k // 2 not in reserved:
                break
        return PP[k // 2][:, (k % 2) * 512:(k % 2 + 1) * 512]

    def bank2(reserve=False):
        while True:
            k = (bstate["b"] + 1) // 2 % 4
            bstate["b"] = (2 * k + 2) % 8
            if k not in reserved:
                break
        if reserve:
            reserved.add(k)
        return PP[k]

    def bank_r():
        b = bank()
        k = (bstate["b"] - 1) % 8 // 2
        reserved.add(k)
        return b

    def bf(ap):
        return ap.bitcast(BF16)

    plan = []

    def wview(w2d, kc, c0, ncols, k0=0):
        return w2d.rearrange("(kc kp) n -> kp kc n", kp=128)[:, k0:k0 + kc, c0:c0 + ncols]

    def plan_ffn(l, which):
        for j in range(11):
            plan.append(("gu", [(wview(w_gu[which][l], 8, 256 * j, 256), 8, 256),
                                (wview(w_gu[which][l], 8, FF + 256 * j, 256), 8, 256)]))
        for h in range(2):
            for u in range(3):
                nk = 8 if u < 2 else 6
                plan.append(("dn", [(wview(w_dn[which][l], nk, 512 * h, 512, k0=8 * u), nk, 512)]))

    def plan_mixer(l):
        wi = w_in[l]
        plan.append(("q", [(wview(wi, 8, O_Q, 512), 8, 512)]))
        plan.append(("k", [(wview(wi, 8, O_K, 512), 8, 512)]))
        for u in range(2):
            plan.append(("v", [(wview(wi, 8, O_V + 512 * u, 512), 8, 512)]))
        for u in range(2):
            plan.append(("g", [(wview(wi, 8, O_G + 512 * u, 512), 8, 512)]))
        plan.append(("a", [(wview(wi, 8, O_A - 112, 128), 8, 128)]))
        if stage_limit < 2:
            return
        plan.append(("dt", [(wview(wi, 8, O_DT - 96, 128), 8, 128)]))
        for g in range(4):
            plan.append(("z", [(wview(wi, 8, O_Z + 512 * g, 512), 8, 512)]))
            plan.append(("xs", [(wview(wi, 8, O_X + 512 * g, 512), 8, 512)]))
            plan.append(("bc", [(wview(wi, 8, O_X + 2048 + 128 * g, 128), 8, 128),
                                (wview(wi, 8, O_X + 2560 + 128 * g, 128), 8, 128)]))
        if stage_limit < 3:
            return
        for u in range(2):
            plan.append(("ga", [(wview(wi, 8, O_GA + 512 * u, 512), 8, 512)]))
        for u in range(2):
            plan.append(("gb", [(wview(wi, 8, O_GB + 512 * u, 512), 8, 512)]))
        for u in range(2):
            plan.append(("pg", [(wview(w_pg[l], 8, 512 * u, 512), 8, 512)]))
        for u in range(4):
            plan.append(("ps", [(wview(w_ps[l], 16, 256 * u, 256), 16, 256)]))
        for u in range(2):
            plan.append(("wo", [(wview(w_o[l], 8, 512 * u, 512), 8, 512)]))

    for p_ in range(last_pass):
        for l in range(NL):
            plan_ffn(l, 0)
            if do_mixer:
                plan_mixer(l)
            plan_ffn(l, 1)

    wst = {"issued": 0, "next": 0}

    def w_issue(i):
        if _NOWDMA:
            return
        tag, parts = plan[i]
        s = i % NSLOT
        off = 0
        for (src, kc, ncols) in parts:
            dst = wring[:, s, off:off + kc * ncols].rearrange("p (k n) -> p k n", k=kc)
            if kc > 8:
                P.dma("pool", dst[:, 0:8, :], src[:, 0:8, :])
                P.dma("pool", dst[:, 8:kc, :], src[:, 8:kc, :])
            else:
                P.dma("pool", dst, src)
            off += kc * ncols

    def wnext(tag):
        i = wst["next"]
        assert plan[i][0] == tag, (plan[i][0], tag)
        while wst["issued"] < min(len(plan), i + NSLOT):
            w_issue(wst["issued"])
            wst["issued"] += 1
        wst["next"] = i + 1
        s = i % NSLOT
        views = []
        off = 0
        for (_, kc, ncols) in plan[i][1]:
            views.append(wring[:, s, off:off + kc * ncols].rearrange("p (k n) -> p k n", k=kc))
            off += kc * ncols
        return views

    P.dma("sp", CON[:, :], consts_in[:, 0:NCONST_SB])
    P.dma("pool", BMt[:, :, :], bm_in.ap().rearrange("p (b t) -> p b t", b=16) if hasattr(bm_in, "ap") else bm_in[:, :].rearrange("p (b t) -> p b t", b=16))
    P.dma("sp", PAR[:, :], params_in[:, :])
    P.copy(identb[:, :], identf)
    P.memset(onesb[:, :], 1.0)
    P.memset(Sst[:, :, :], 0.0)
    P.memset(hT[:, :, :], 0.0)
    P.memset(halo[:, :, :, :], 0.0)
    P.act(expA[:, :], par(P_ALOG, 64), AF.Exp)

    def rms_T(tiles, ncol_off, l):
        ncol = PAR[:, ncol_off + 8 * l: ncol_off + 8 * l + 8]
        for (kind, n, c0, ti) in tiles:
            xt = x[:n, ti, :]
            P.act(junk[:n, :], xt, AF.Square, accum_out=st1[:n, 0:1])
            P.ts(st2[:n, 0:1], st1[:n, 0:1], 1.0 / D, ALU.mult, EPS, ALU.add)
            P.act(st2[:n, 0:1], st2[:n, 0:1], AF.Ln); P.act(st2[:n, 0:1], st2[:n, 0:1], AF.Exp, scale=-0.5)
            P.act(xsb[:n, :], xt, AF.Copy, scale=st2[:n, 0:1])
            pt = bf(bank()).rearrange("p (c n) -> p c n", c=8)
            for c in range(8):
                P.tr(pt[:, c, :n], xsb[:n, c * 128:(c + 1) * 128], identb[:n, :n])
            P.tt(xnT[:, :, c0:c0 + n], pt[:, :, :n], ncol.unsqueeze(2).to_broadcast([128, 8, n]), ALU.mult)

    def ffn(tiles, blocks, l, which):
        rms_T(tiles, P_NF1 if which == 0 else P_NF2, l)
        actT = P.sb("actT", [128, 22, T], BF16, at=A0)
        sgt = P.sb("sgt", [128, 2, 512], F32, at=A0 + 22 * T * 2)
        k = 0
        for j in range(11):
            wg, wu = wnext("gu")
            for fc in range(2):
                f = 2 * j + fc
                for (c0, c1) in blocks:
                    n = c1 - c0
                    pg = bank()
                    pu = bank()
                    for kc in range(8):
                        P.mm(pg[:, :n], wg[:, kc, fc * 128:(fc + 1) * 128], xnT[:, kc, c0:c1],
                             start=(kc == 0), stop=(kc == 7))
                    for kc in range(8):
                        P.mm(pu[:, :n], wu[:, kc, fc * 128:(fc + 1) * 128], xnT[:, kc, c0:c1],
                             start=(kc == 0), stop=(kc == 7))
                    s = sgt[:, k % 2, :n]
                    k += 1
                    P.act(s, pg[:, :n], AF.Silu)
                    P.tt(actT[:, f, c0:c1], s, pu[:, :n], ALU.mult)
        for h in range(2):
            po = [bank() for _ in tiles]
            for u in range(3):
                (wd,) = wnext("dn")
                nk = 8 if u < 2 else 6
                for i, (kind, n, c0, ti) in enumerate(tiles):
                    for kk in range(nk):
                        P.mm(po[i][:n, :], actT[:, 8 * u + kk, c0:c0 + n], wd[:, kk, :],
                             start=(u == 0 and kk == 0), stop=(u == 2 and kk == nk - 1))
            for i, (kind, n, c0, ti) in enumerate(tiles):
                xv = x[:n, ti, 512 * h:512 * (h + 1)]
                P.stt(xv, po[i][:n, :], 0.5, xv, ALU.mult, ALU.add)

    def mixer(tiles, blocks, l, pi):
        rms_T(tiles, P_NMIX, l)
        last = (pi == NPASS - 1)
        ogT = P.sb("ogT", [128, 8, T], BF16, at=A0)
        ynT = P.sb("ynT", [128, 16, T], BF16, at=A0 + 8 * T * 2)
        rstdbc = P.sb("rstdbc", [128, T], F32, at=A0 + 24 * T * 2)
        G0 = A0 + 24 * T * 2 + T * 4
        ar = {"o": G0}

        def tmp(name, shape, dtype):
            t = P.sb(name, shape, dtype, at=ar["o"])
            row = int(np.prod(shape[1:])) * _DS[dtype]
            ar["o"] = (ar["o"] + row + 63) // 64 * 64
            return t

        def proj_fm(w, ncol_chunks, evac):
            for c in range(ncol_chunks):
                for (c0, c1) in blocks:
                    n = c1 - c0
                    pb = bank()
                    for kc in range(8):
                        P.mm(pb[:, :n], w[:, kc, c * 128:(c + 1) * 128], xnT[:, kc, c0:c1],
                             start=(kc == 0), stop=(kc == 7))
                    evac(c, c0, c1, pb[:, :n])

        def proj_tm(w, ncols, evac):
            for tl in tiles:
                (kind, n, c0, ti) = tl
                pb = bank()
                for kc in range(8):
                    P.mm(pb[:n, :ncols], xnT[:, kc, c0:c0 + n], w[:, kc, :ncols],
                         start=(kc == 0), stop=(kc == 7))
                evac(tl, pb[:n, :ncols])

        qT = tmp("qT", [128, 4, T], BF16)
        kT = tmp("kT", [128, 4, T], BF16)
        ktok = tmp("ktok", [128, 5, 512], BF16)
        vtok = tmp("vtok", [128, 5, 1024], BF16)
        sgl = tmp("sgl", [128, 5, 1024], BF16)
        alrT = tmp("alrT", [128, T], F32)
        e_ = tmp("e_", [128, 512], F32)
        eb = tmp("eb", [128, 4, 128], F32)
        enb = tmp("enb", [128, 4, 128], F32)
        esuf = tmp("esuf", [128, 512], F32)
        qin = tmp("qin", [128, 4, 128], BF16)
        kin = tmp("kin", [128, 4, 128], BF16)
        kout = tmp("kout", [128, 512], BF16)
        scm = tmp("scm", [128, 4, 128], BF16)
        og = tmp("og", [128, 1024], BF16)
        has_s = any(t[0] == "s" for t in tiles)
        if has_s:
            otmp = tmp("otmp", [128, 1024], F32)
            qmb = tmp("qmb", [128, 2, 4, 64], F32)
            qinf = tmp("qinf", [128, 4, 64], F32)
            koutm = tmp("koutm", [128, 2, 512], BF16)
            S0 = tmp("S0", [128, 1024], F32)
            Sout = tmp("Sout", [128, 1024], F32)

        (wq,) = wnext("q")
        proj_fm(wq, 4, lambda c, c0, c1, ps: P.copy(qT[:, c, c0:c1], ps, eng="act"))
        (wk,) = wnext("k")
        proj_fm(wk, 4, lambda c, c0, c1, ps: P.copy(kT[:, c, c0:c1], ps, eng="act"))
        proj_tm(wk, 512, lambda tl, ps: P.copy(ktok[:tl[1], tl[3], :], ps))
        for u in range(2):
            (wv,) = wnext("v")
            proj_tm(wv, 512, lambda tl, ps, u=u: P.copy(vtok[:tl[1], tl[3], 512 * u:512 * (u + 1)], ps))
        for u in range(2):
            (wg_,) = wnext("g")
            proj_tm(wg_, 512, lambda tl, ps, u=u: P.act(sgl[:tl[1], tl[3], 512 * u:512 * (u + 1)], ps, AF.Silu))
        (wa,) = wnext("a")
        for (c0, c1) in blocks:
            n = c1 - c0
            pb = bank()
            for kc in range(8):
                P.mm(pb[:16, :n], wa[:, kc, 112:128], xnT[:, kc, c0:c1], start=(kc == 0), stop=(kc == 7))
            P.copy(alrT[:16, c0:c1], pb[:16, :n])

        w2 = PAR[:16, P_W2 + 512 * l:P_W2 + 512 * (l + 1)]
        bgla = PAR[:, P_BGLA + 512 * l:P_BGLA + 512 * (l + 1)]
        glan = PAR[:, P_GLAN + 8 * l:P_GLAN + 8 * l + 8]

        P.copy(Sbf[:, :], Sst[:, l, :], eng="act")
        for (kind, n, c0, ti) in tiles:
            cs = slice(c0, c0 + n)
            if kind == "p":
                CAUS, NCAUS, NSUF = cv(C_CAUS, n), cv(C_NCAUS, n), cv(C_NSUF, n)
            else:
                CAUS, NCAUS, NSUF = cv(C_CAUSS, n, 64), cv(C_NCAUSS, n, 64), cv(C_NSUFS, n, 64)
            pb = bank()
            P.mm(pb[:n, :], alrT[:16, cs], w2, start=True, stop=True)
            P.tt(e_[:n, :], pb[:n, :], bgla[:n, :], ALU.add)
            P.act(e_[:n, :], e_[:n, :], AF.Exp, scale=-1.0)
            P.act(e_[:n, :], e_[:n, :], AF.Ln, bias=1.0)
            P.ts(e_[:n, :], e_[:n, :], 1.0 / 16.0, ALU.mult)
            pc = bank().rearrange("p (h n) -> p h n", h=4)
            for h in range(4):
                P.mm(pc[:, h, :n], e_[:n, h * 128:(h + 1) * 128], NCAUS)
            psf = bank()
            P.mm(psf[:n, :], NSUF, e_[:n, :])
            P.act(eb[:, :, :n], pc[:, :, :n], AF.Exp)
            P.act(enb[:, :, :n], pc[:, :, :n], AF.Exp, scale=-1.0)
            P.act(esuf[:n, :], psf[:n, :], AF.Exp)
            P.stt(qin[:, :, :n], qT[:, :, cs], 128.0 ** -0.5, eb[:, :, :n], ALU.mult, ALU.mult)
            P.tt(kin[:, :, :n], kT[:, :, cs], enb[:, :, :n], ALU.mult)
            P.tt(kout[:n, :], ktok[:n, ti, :], esuf[:n, :], ALU.mult)
            psc = bank().rearrange("p (h n) -> p h n", h=4)
            for h in range(4):
                P.mm(psc[:n, h, :n], kin[:, h, :n], qin[:, h, :n])
            P.tt(scm[:n, :, :n], psc[:n, :, :n], CAUS.unsqueeze(1).to_broadcast([n, 4, n]), ALU.mult)
            if kind == "p":
                po = bank2()
                for h in range(4):
                    hs = slice(h * 256, (h + 1) * 256)
                    P.mm(po[:n, hs], scm[:n, h, :n], vtok[:n, ti, hs], start=True, stop=False)
                    P.mm(po[:n, hs], qin[:, h, :n], Sbf[:, hs], start=False, stop=True)
                pds = bank2()
                for h in range(4):
                    hs = slice(h * 256, (h + 1) * 256)
                    P.mm(pds[:, hs], kout[:n, h * 128:(h + 1) * 128], vtok[:n, ti, hs])
                for h in range(4):
                    hs = slice(h * 256, (h + 1) * 256)
                    P.stt(Sst[:, l, hs], Sst[:, l, hs], eb[:, h, n - 1:n], pds[:, hs], ALU.mult, ALU.add)
                P.copy(Sbf[:, :], Sst[:, l, :], eng="act")
                osrc = po
            else:
                po = bank2(reserve=True)
                pA = bank2(reserve=True)
                pB = bank2(reserve=True)
                pih = [pA[:, 0:256], pA[:, 512:768], pB[:, 0:256], pB[:, 512:768]]
                P.stt(qinf[:, :, :n], qT[:, :, cs], 128.0 ** -0.5, eb[:, :, :n], ALU.mult, ALU.mult)
                for h in range(4):
                    hs = slice(h * 256, (h + 1) * 256)
                    P.mm(po[:n, hs], scm[:n, h, :n], vtok[:n, ti, hs], start=True, stop=True)
                for b in range(16):
                    s0 = S0[:, :]
                    P.dma("sp", s0.rearrange("p (h v) -> p h v", h=4), sg_in[l, b].rearrange("h d v -> d h v"))
                    qm = qmb[:, b % 2, :, :]
                    P.tt(qm, qinf[:, :, :], BMt[:, b, :].unsqueeze(1).to_broadcast([128, 4, 64]), ALU.mult)
                    for h in range(4):
                        hs = slice(h * 256, (h + 1) * 256)
                        P.mm(pih[h][:n, :], qm[:, h, :], s0[:, hs], start=(b == 0), stop=(b == 15))
                    km = koutm[:n, b % 2, :]
                    P.ts(km, kout[:n, :], CON[:n, C_BMCOL + b:C_BMCOL + b + 1], ALU.mult)
                    pds = bank2()
                    so = Sout[:, :]
                    for h in range(4):
                        hs = slice(h * 256, (h + 1) * 256)
                        P.mm(pds[:, hs], km[:, h * 128:(h + 1) * 128], vtok[:n, ti, hs])
                    for h in range(4):
                        hs = slice(h * 256, (h + 1) * 256)
                        P.stt(so[:, hs], s0[:, hs], eb[:, h, 4 * b + 3:4 * b + 4], pds[:, hs], ALU.mult, ALU.add)
                    P.dma("sp", gso[l, b].rearrange("h d v -> d h v"), so.rearrange("p (h v) -> p h v", h=4))
                for h in range(4):
                    P.copy(otmp[:n, h * 256:(h + 1) * 256], pih[h][:n, :], eng="act")
                dump(4, otmp[:n, 0:512], 64, 512)
                P.tt(otmp[:n, :], otmp[:n, :], po[:n, :], ALU.add)
                dump(0, otmp[:n, 0:512], 64, 512)
                dump(1, otmp[:n, 512:1024], 64, 512)
                reserved.clear()
                osrc = otmp
            for h in range(4):
                P.act(junk[:n, :256], osrc[:n, h * 256:(h + 1) * 256], AF.Square, accum_out=st1[:n, h:h + 1])
            P.ts(st2[:n, 0:4], st1[:n, 0:4], 1.0 / 256.0, ALU.mult, EPS, ALU.add)
            P.act(st2[:n, 0:4], st2[:n, 0:4], AF.Ln); P.act(st2[:n, 0:4], st2[:n, 0:4], AF.Exp, scale=-0.5)
            for h in range(4):
                hs = slice(h * 256, (h + 1) * 256)
                P.stt(og[:n, hs], osrc[:n, hs], st2[:n, h:h + 1], sgl[:n, ti, hs], ALU.mult, ALU.mult)
            pt = bf(bank()).rearrange("p (c n) -> p c n", c=8)
            for c in range(8):
                P.tr(pt[:, c, :n], og[:n, c * 128:(c + 1) * 128], identb[:n, :n])
            P.tt(ogT[:, :, cs], pt[:, :, :n], glan.unsqueeze(2).to_broadcast([128, 8, n]), ALU.mult)

        if last:
            P.dma("sp", gp[l].rearrange("h d v -> d h v"), Sst[:, l, :].rearrange("p (h v) -> p h v", h=4))

        if stage_limit < 2:
            return
        ar["o"] = G0
        dtall = tmp("dtall", [128, 5, 32], F32)
        rall = tmp("rall", [128, 5, 32], F32)
        eaall = tmp("eaall", [128, 5, 32], F32)
        esall = tmp("esall", [128, 5, 32], F32)
        decall = tmp("decall", [128, 5, 32], F32)
        ssq = tmp("ssq", [128, 5, 4], F32)
        szg = tmp("szg", [128, 5, 512], BF16)
        xcT = tmp("xcT", [128, 6, T], BF16)
        pre = tmp("pre", [128, 2, 3 + TP], BF16)
        acc = tmp("acc", [128, 512], F32)
        rbig = tmp("rbig", [128, 8, 128], F32)
        seg = tmp("seg", [128, 8, 128], F32)
        MT = tmp("MT", [128, 8, 128], BF16)
        cbm = tmp("cbm", [128, 128], F32)
        xstok = tmp("xstok", [128, 512], BF16)
        Btok = tmp("Btok", [128, 128], BF16)
        xg = tmp("xg", [128, 512], BF16)
        xgw = tmp("xgw", [128, 512], BF16)
        t1 = tmp("t1", [128, 512], F32)
        t2 = tmp("t2", [128, 512], F32)
        yz = tmp("yz", [128, 512], BF16)
        cvt = tmp("cvt", [128, 2, 512], F32)
        if has_s:
            ext = tmp("ext", [128, 2, 16, 7], BF16)
            accs = tmp("accs", [128, 16, 4], F32)
            sctok = tmp("sctok", [128, 768], F32)
            scT = tmp("scT", [128, 6, 16, 3], BF16)
            sctokb = tmp("sctokb", [128, 768], BF16)
            CTm = tmp("CTm", [128, 16, 64], BF16)
            h0b = tmp("h0b", [128, 2, 512], BF16)
            rrep = tmp("rrep", [128, 8, 64], F32)
            dcol = tmp("dcol", [128, 4, 16], F32)
            xgwm = tmp("xgwm", [128, 2, 512], BF16)
            h0 = tmp("h0", [128, 2, 4, 128], F32)
            h0T = tmp("h0T", [128, 2, 512], BF16)
            hout = tmp("hout", [128, 2, 4, 128], F32)
        assert ar["o"] <= 229376, ar["o"]

        dtb = PAR[:, P_DTB + 32 * l:P_DTB + 32 * (l + 1)]
        eA = expA[:, 32 * l:32 * (l + 1)]
        dsk = PAR[:, P_DSK + 32 * l:P_DSK + 32 * (l + 1)]
        ssmn = PAR[:, P_SSMN + 16 * l:P_SSMN + 16 * (l + 1)]
        cw = PAR[:, P_CW + 96 * l:P_CW + 96 * (l + 1)].rearrange("p (c i) -> p c i", i=4)
        cb = PAR[:, P_CB + 24 * l:P_CB + 24 * (l + 1)]

        P.copy(hTbf[:, :], hT[:, l, :], eng="act")
        (wdt,) = wnext("dt")

        def dt_evac(tl, ps):
            (kind, n, c0, ti) = tl
            if kind == "p":
                NCAUS, NSUF = cv(C_NCAUS, n), cv(C_NSUF, n)
            else:
                NCAUS, NSUF = cv(C_NCAUSS, n, 64), cv(C_NSUFS, n, 64)
            d = dtall[:n, ti, :]
            P.tt(d, ps, dtb[:n, :], ALU.add)
            P.act(d, d, AF.Exp)
            P.act(d, d, AF.Ln, bias=1.0)
            r = rall[:n, ti, :]
            P.tt(r, d, eA[:n, :], ALU.mult)
            p1 = bank()
            P.mm(p1[:n, 0:32], NCAUS, r)
            P.act(eaall[:n, ti, :], p1[:n, 0:32], AF.Exp)
            p2 = bank()
            P.mm(p2[:n, 0:32], NSUF, r)
            P.act(esall[:n, ti, :], p2[:n, 0:32], AF.Exp)
            if kind == "p":
                p3 = bank()
                P.mm(p3[:, 0:32], cv(C_ALLM1, 128), r)
                P.act(decall[:, ti, :], p3[:, 0:32], AF.Exp)

        proj_tm(wdt[:, :, 96:128], 32, dt_evac)

        def conv_chunk(j, ch, c_p, c_s, k):
            if c_p is not None:
                pr = pre[:, k % 2, :]
                P.copy(pr[:, 0:3], halo[:, l, ch, :])
                P.copy(pr[:, 3:3 + TP], c_p, eng="act")
                P.ts(acc[:, :], pr[:, 3:3 + TP], cw[:, ch, 3:4], ALU.mult, cb[:, ch:ch + 1], ALU.add)
                for i in range(3):
                    P.stt(acc[:, :], pr[:, i:i + TP], cw[:, ch, i:i + 1], acc[:, :], ALU.mult, ALU.add)
                P.act(xcT[:, j, 0:TP], acc[:, :], AF.Silu)
                P.copy(halo[:, l, ch, :], pr[:, TP:TP + 3])
            if c_s is not None:
                ex = ext[:, k % 2, :, :]
                P.copy(ex[:, :, 0:3], scT[:, j, :, :])
                P.copy(ex[:, :, 3:7], c_s.rearrange("p (b t) -> p b t", b=16), eng="act")
                P.ts(accs[:, :, :], ex[:, :, 3:7], cw[:, ch, 3:4], ALU.mult, cb[:, ch:ch + 1], ALU.add)
                for i in range(3):
                    P.stt(accs[:, :, :], ex[:, :, i:i + 4], cw[:, ch, i:i + 1], accs[:, :, :], ALU.mult, ALU.add)
                P.act(xcT[:, j, TP:T].rearrange("p (b t) -> p b t", b=16), accs[:, :, :], AF.Silu)

        kconv = [0]
        ncv = [0]

        def conv_unit(w, nch, jbase, chs, colbase):
            for c in range(nch):
                c_p = c_s = None
                for (c0, c1) in blocks:
                    n = c1 - c0
                    pb = bank()
                    for kc in range(8):
                        P.mm(pb[:, :n], w[:, kc, c * 128:(c + 1) * 128], xnT[:, kc, c0:c1],
                             start=(kc == 0), stop=(kc == 7))
                    if c0 == 0:
                        c_p = pb[:, :TP]
                    else:
                        c_s = pb[:, :TS]
                conv_chunk(jbase + c, chs[c], c_p, c_s, kconv[0])
                kconv[0] += 1
            ncols = nch * 128
            if has_s:
                pb = bank()
                for kc in range(8):
                    P.mm(pb[:TS, :ncols], xnT[:, kc, TP:T], w[:, kc, :ncols], start=(kc == 0), stop=(kc == 7))
                cvb = cvt[:, ncv[0] % 2, :]
                ncv[0] += 1
                P.copy(cvb[:TS, :ncols], pb[:TS, :ncols])
                for jj in range(3):
                    P.dma("sp", cso[l, :, jj, colbase:colbase + ncols], cvb[1 + jj:TS:4, :ncols])
            if last:
                pb = bank()
                for kc in range(8):
                    P.mm(pb[:32, :ncols], xnT[:, kc, TP - 32:TP], w[:, kc, :ncols], start=(kc == 0), stop=(kc == 7))
                cvb = cvt[:, ncv[0] % 2, :]
                ncv[0] += 1
                P.copy(cvb[:32, :ncols], pb[:32, :ncols])
                P.dma("sp", cpo[l, :, colbase:colbase + ncols], cvb[29:32, :ncols])

        for g in range(4):
            gh = slice(8 * g, 8 * g + 8)
            (wz,) = wnext("z")
            proj_tm(wz, 512, lambda tl, ps: P.act(szg[:tl[1], tl[3], :], ps, AF.Silu))
            if has_s:
                scv = sc_in[l].rearrange("b j c -> (b j) c")
                P.dma("sp", sctok[:48, 0:512], scv[:, 512 * g:512 * (g + 1)])
                P.dma("sp", sctok[:48, 512:640], scv[:, 2048 + 128 * g:2048 + 128 * (g + 1)])
                P.dma("sp", sctok[:48, 640:768], scv[:, 2560 + 128 * g:2560 + 128 * (g + 1)])
                P.copy(sctokb[:48, :], sctok[:48, :])
                pt = bf(bank()).rearrange("p (c n) -> p c n", c=8)
                for c in range(6):
                    P.tr(pt[:, c, :48], sctokb[:48, c * 128:(c + 1) * 128], identb[:48, :48])
                P.copy(scT[:, :, :, :].rearrange("p c b j -> p c (b j)"), pt[:, 0:6, :48])
            (wx,) = wnext("xs")
            conv_unit(wx, 4, 0, [4 * g + c for c in range(4)], 512 * g)
            wb_, wc_ = wnext("bc")
            conv_unit(wb_, 1, 4, [16 + g], 2048 + 128 * g)
            conv_unit(wc_, 1, 5, [20 + g], 2560 + 128 * g)

            for (kind, n, c0, ti) in tiles:
                cs = slice(c0, c0 + n)
                if kind == "p":
                    CAUS, NSUF = cv(C_CAUS, n), cv(C_NSUF, n)
                else:
                    CAUS, NSUF = cv(C_CAUSS, n, 64), cv(C_NSUFS, n, 64)
                pcb = bank()
                P.mm(pcb[:n, :n], xcT[:, 4, cs], xcT[:, 5, cs])
                P.tt(cbm[:n, :n], pcb[:n, :n], CAUS, ALU.mult)
                P.tt(rbig[:n, :, :n], rall[:n, ti, gh].unsqueeze(2).to_broadcast([n, 8, n]),
                     CAUS.unsqueeze(1).to_broadcast([n, 8, n]), ALU.mult)
                pD = bank2().rearrange("p (h n) -> p h n", h=8)
                for hf in range(2):
                    P.mm(pD[:n, 4 * hf:4 * hf + 4, :n], NSUF, rbig[:n, 4 * hf:4 * hf + 4, :n])
                P.act(seg[:n, :, :n], pD[:n, :, :n], AF.Exp)
                P.tt(MT[:n, :, :n], seg[:n, :, :n], cbm[:n, :n].unsqueeze(1).to_broadcast([n, 8, n]), ALU.mult)
                ptb = bf(bank()).rearrange("p (c n) -> p c n", c=8)
                for j in range(5):
                    P.tr(ptb[:n, j, :], xcT[:, j, cs], identb[:, :])
                P.copy(xstok[:n, :].rearrange("p (c n) -> p c n", c=4), ptb[:n, 0:4, :], eng="act")
                P.copy(Btok[:n, :], ptb[:n, 4, :])
                xs3 = xstok[:n, :].rearrange("p (h q) -> p h q", h=8)
                P.tt(xg[:n, :].rearrange("p (h q) -> p h q", h=8), xs3,
                     dtall[:n, ti, gh].unsqueeze(2).to_broadcast([n, 8, 64]), ALU.mult)
                P.tt(xgw[:n, :].rearrange("p (h q) -> p h q", h=8), xg[:n, :].rearrange("p (h q) -> p h q", h=8),
                     esall[:n, ti, gh].unsqueeze(2).to_broadcast([n, 8, 64]), ALU.mult)
                pyd = bank_r() if kind == "s" else bank()
                for hh in range(8):
                    P.mm(pyd[:n, hh * 64:(hh + 1) * 64], MT[:n, hh, :n], xg[:n, hh * 64:(hh + 1) * 64])
                pyo = bank_r() if kind == "s" else bank()
                if kind == "p":
                    P.mm(pyo[:n, :], xcT[:, 5, cs], hTbf[:, 512 * g:512 * (g + 1)])
                else:
                    BM = BMt[:, :, :]
                    P.tt(CTm[:, :, :], xcT[:, 5, cs].unsqueeze(1).to_broadcast([128, 16, 64]), BM, ALU.mult)
                    P.memset(Btok[64:128, :], 0.0)
                    P.memset(xgwm[64:128, :, :], 0.0)
                    P.copy(rrep[:n, :, :], rall[:n, ti, gh].unsqueeze(2).to_broadcast([n, 8, 64]))
                    pdc = bank().rearrange("p (j b) -> p j b", j=4)
                    for j in range(4):
                        P.mm(pdc[:, j, 0:16], rrep[:n, 2 * j:2 * j + 2, :].rearrange("p h q -> p (h q)"),
                             CON[:n, C_NSEQ:C_NSEQ + 16])
                    P.act(dcol[:, :, :], pdc[:, :, 0:16], AF.Exp)
                    h0src = lambda b: ss_in[l, b, gh].rearrange("h q d -> (h q) d").rearrange("(j p) d -> p j d", p=128)
                    pT = bf(bank()).rearrange("p (j n) -> p j n", j=8)
                    pds = bank().rearrange("p (j n) -> p j n", j=4)
                    for b in range(16):
                        hb = h0[:, b % 2, :, :]
                        P.dma("sp", hb, h0src(b))
                        xm = xgwm[:n, b % 2, :]
                        P.ts(xm, xgw[:n, :], CON[:n, C_BMCOL + b:C_BMCOL + b + 1], ALU.mult)
                        ho = hout[:, b % 2, :, :]
                        for j in range(4):
                            P.mm(pds[:, j, :], xgwm[:, b % 2, j * 128:(j + 1) * 128], Btok[:, :])
                        for j in range(4):
                            P.stt(ho[:, j, :], hb[:, j, :], dcol[:, j, b:b + 1], pds[:, j, :], ALU.mult, ALU.add)
                        P.dma("sp", sso[l, b, gh].rearrange("h q d -> (h q) d").rearrange("(j p) d -> p j d", p=128), ho)
                    for b in range(16):
                        hb = h0[:, b % 2, :, :]
                        P.dma("sp", hb, h0src(b))
                        hbT = h0T[:, b % 2, :]
                        hbb = h0b[:, b % 2, :]
                        P.copy(hbb, hb.rearrange("p j n -> p (j n)"), eng="act")
                        for j in range(4):
                            P.tr(pT[:, j, :], hbb[:, j * 128:(j + 1) * 128], identb[:, :])
                        P.copy(hbT.rearrange("p (j n) -> p j n", j=4), pT[:, 0:4, :], eng="act")
                        P.mm(pyo[:n, :], CTm[:, b, :], hbT, start=(b == 0), stop=(b == 15))
                reserved.clear()
                t13 = t1[:n, :].rearrange("p (h q) -> p h q", h=8)
                P.tt(t13, pyo[:n, :].rearrange("p (h q) -> p h q", h=8),
                     eaall[:n, ti, gh].unsqueeze(2).to_broadcast([n, 8, 64]), ALU.mult)
                if kind == "s":
                    dump(3, t1[:n, :], 64, 512)
                P.tt(t1[:n, :], t1[:n, :], pyd[:n, :], ALU.add)
                if kind == "s":
                    dump(2, t1[:n, :], 64, 512)
                P.tt(t2[:n, :].rearrange("p (h q) -> p h q", h=8), xs3,
                     dsk[:n, gh].unsqueeze(2).to_broadcast([n, 8, 64]), ALU.mult)
                P.tt(t1[:n, :], t1[:n, :], t2[:n, :], ALU.add)
                P.tt(t1[:n, :], t1[:n, :], szg[:n, ti, :], ALU.mult)
                P.act(junk[:n, :512], t1[:n, :], AF.Square, accum_out=ssq[:n, ti, g:g + 1])
                P.copy(yz[:n, :], t1[:n, :], eng="act")
                pty = bf(bank()).rearrange("p (c n) -> p c n", c=8)
                for j in range(4):
                    P.tr(pty[:, j, :n], yz[:n, j * 128:(j + 1) * 128], identb[:n, :n])
                P.tt(ynT[:, 4 * g:4 * g + 4, cs], pty[:, 0:4, :n],
                     ssmn[:, 4 * g:4 * g + 4].unsqueeze(2).to_broadcast([128, 4, n]), ALU.mult)
                if kind == "p":
                    pdS = bank()
                    P.mm(pdS[:, :], Btok[:n, :], xgw[:n, :])
                    hv = hT[:, l, 512 * g:512 * (g + 1)]
                    P.tt(t2[:, :].rearrange("p (h q) -> p h q", h=8), hv.rearrange("p (h q) -> p h q", h=8),
                         decall[:, ti, gh].unsqueeze(2).to_broadcast([128, 8, 64]), ALU.mult)
                    P.tt(hv, t2[:, :], pdS[:, :], ALU.add)
                    P.copy(hTbf[:, 512 * g:512 * (g + 1)], hv, eng="act")

        for (kind, n, c0, ti) in tiles:
            P.reduce_sum(st1[:n, 0:1], ssq[:n, ti, :])
            P.ts(st2[:n, 0:1], st1[:n, 0:1], 1.0 / 2048.0, ALU.mult, EPS, ALU.add)
            P.act(st2[:n, 0:1], st2[:n, 0:1], AF.Ln); P.act(st2[:n, 0:1], st2[:n, 0:1], AF.Exp, scale=-0.5)
            P.copy(hl[:n, 0:1], st2[:n, 0:1])
            P.copy(hlf[:n, 0:1], hl[:n, 0:1])
            P.tt(hl[:n, 1:2], st2[:n, 0:1], hlf[:n, 0:1], ALU.subtract)
            pr_ = bf(bank())
            P.tr(pr_[0:1, 0:n], hl[:n, 0:1], identb[:n, :n])
            P.tr(pr_[0:1, 128:128 + n], hl[:n, 1:2], identb[:n, :n])
            P.copy(rrow[0:1, 0:256], pr_[0:1, 0:256])
            pb = bank()
            P.mm(pb[:, :n], onesb[0:1, :128], rrow[0:1, 0:n], start=True, stop=False)
            P.mm(pb[:, :n], onesb[0:1, :128], rrow[0:1, 128:128 + n], start=False, stop=True)
            P.copy(rstdbc[:, c0:c0 + n], pb[:, :n], eng="act")

        if last:
            hn = tmp("hn", [128, 4, 128], F32)
            hlo = tmp("hlo", [128, 512], BF16)
            hhf = tmp("hhf", [128, 512], F32)
            for q4 in range(4):
                qs = slice(512 * q4, 512 * (q4 + 1))
                P.copy(hhf[:, :], hTbf[:, qs])
                P.tt(hlo[:, :], hT[:, l, qs], hhf[:, :], ALU.subtract)
                pT = bf(bank()).rearrange("p (j n) -> p j n", j=8)
                for j in range(4):
                    jj = 4 * q4 + j
                    P.tr(pT[:, j, :], hTbf[:, jj * 128:(jj + 1) * 128], identb[:, :])
                    P.tr(pT[:, 4 + j, :], hlo[:, j * 128:(j + 1) * 128], identb[:, :])
                P.copy(hn[:, :, :], pT[:, 0:4, :])
                P.tt(hn[:, :, :], hn[:, :, :], pT[:, 4:8, :], ALU.add)
                P.dma("sp", spo[l].rearrange("h q d -> (h q) d").rearrange("(j p) d -> p j d", p=128)[:, 4 * q4:4 * q4 + 4, :],
                      hn[:, :, :])

        if stage_limit < 3:
            return
        ar["o"] = G0
        sgaT = tmp("sgaT", [128, 8, T], BF16)
        sgbT = tmp("sgbT", [128, 8, T], BF16)
        mT = tmp("mT", [128, 8, T], BF16)
        tA = tmp("tA", [128, 2, 512], F32)
        for u in range(2):
            (wga,) = wnext("ga")
            proj_fm(wga, 4, lambda c, c0, c1, ps, u=u: P.act(sgaT[:, 4 * u + c, c0:c1], ps, AF.Sigmoid))
        for u in range(2):
            (wgb,) = wnext("gb")
            proj_fm(wgb, 4, lambda c, c0, c1, ps, u=u: P.act(sgbT[:, 4 * u + c, c0:c1], ps, AF.Sigmoid))
        for u in range(2):
            (wpg,) = wnext("pg")
            for c in range(4):
                dc = 4 * u + c
                for (c0, c1) in blocks:
                    n = c1 - c0
                    pb = bank()
                    for kc in range(8):
                        P.mm(pb[:, :n], wpg[:, kc, c * 128:(c + 1) * 128], ogT[:, kc, c0:c1],
                             start=(kc == 0), stop=(kc == 7))
                    P.tt(mT[:, dc, c0:c1], pb[:, :n], sgaT[:, dc, c0:c1], ALU.mult)
        ka = 0
        for u in range(4):
            (wps,) = wnext("ps")
            for c in range(2):
                dc = 2 * u + c
                for (c0, c1) in blocks:
                    n = c1 - c0
                    pb = bank()
                    for kc in range(16):
                        P.mm(pb[:, :n], wps[:, kc, c * 128:(c + 1) * 128], ynT[:, kc, c0:c1],
                             start=(kc == 0), stop=(kc == 15))
                    ta = tA[:, ka % 2, :n]
                    ka += 1
                    P.tt(ta, pb[:, :n], rstdbc[:, c0:c1], ALU.mult)
                    P.tt(ta, ta, sgbT[:, dc, c0:c1], ALU.mult)
                    P.tt(mT[:, dc, c0:c1], mT[:, dc, c0:c1], ta, ALU.add)
        for u in range(2):
            (wo,) = wnext("wo")
            for (kind, n, c0, ti) in tiles:
                pb = bank()
                for kc in range(8):
                    P.mm(pb[:n, :], mT[:, kc, c0:c0 + n], wo[:, kc, :], start=(kc == 0), stop=(kc == 7))
                xv = x[:n, ti, 512 * u:512 * (u + 1)]
                P.tt(xv, xv, pb[:n, :], ALU.add)

    for pi in range(last_pass):
        tiles = [("p", 128, 128 * i, i) for i in range(4)]
        blocks = [(0, TP)]
        if pi == 0:
            tiles.append(("s", TS, TP, 4))
            blocks.append((TP, T))
        for i in range(4):
            P.dma("sp", x[:, i, :], xp[pi * TP + 128 * i: pi * TP + 128 * (i + 1), :])
        if pi == 0:
            P.dma("sp", x[:TS, 4, :], xs_in[:, :])
        for l in range(NL):
            ffn(tiles, blocks, l, 0)
            if do_mixer:
                mixer(tiles, blocks, l, pi)
            ffn(tiles, blocks, l, 1)
        yo = P.sb("yo", [128, 2, D], F32, at=A0)
        nfin = P.sb("nfin", [128, D], F32, at=A0 + 2 * D * 4)
        P.dma("sp", nfin[:, :], nfin_in[:, :])
        for i, (kind, n, c0, ti) in enumerate(tiles):
            xt = x[:n, ti, :]
            P.act(junk[:n, :], xt, AF.Square, accum_out=st1[:n, 0:1])
            P.ts(st2[:n, 0:1], st1[:n, 0:1], 1.0 / D, ALU.mult, EPS, ALU.add)
            P.act(st2[:n, 0:1], st2[:n, 0:1], AF.Ln); P.act(st2[:n, 0:1], st2[:n, 0:1], AF.Exp, scale=-0.5)
            yv = yo[:n, i % 2, :]
            P.stt(yv, xt, st2[:n, 0:1], nfin[:n, :], ALU.mult, ALU.mult)
            if kind == "p":
                P.dma("sp", yp[pi * TP + c0: pi * TP + c0 + n, :], yv)
            else:
                P.dma("sp", ys[:, :], yv)

    P.emit()
    return nc, P


_CACHE = {}


def kernel(**inputs):
    inp = {k: np.ascontiguousarray(np.asarray(v, dtype=np.float32)) for k, v in inputs.items()}
    if "nc" not in _CACHE:
        _CACHE["nc"] = build()[0]
    nc = _CACHE["nc"]
    consts = make_consts()
    params = make_params(inp)
    nfin_b = np.ascontiguousarray(np.broadcast_to(inp["norm_final"].reshape(1, D), (128, D)))
    bm_b = np.ascontiguousarray(consts[:, C_BM:C_BM + 1024])
    shared = {k: inp[k] for k in ("w_ffn1_gu", "w_ffn2_gu", "w_ffn1_down", "w_ffn2_down", "w_in",
                                  "w_proj_gla", "w_proj_ssm", "w_out")}
    in_maps = []
    for c in range(8):
        m = dict(shared)
        m["consts"] = consts
        m["params"] = params
        m["nfin"] = nfin_b
        m["bm"] = bm_b
        m["xp"] = inp["x_prompt"][c]
        m["xs"] = np.ascontiguousarray(inp["x_sample"][16 * c:16 * (c + 1)].reshape(TS, D))
        m["sg"] = np.ascontiguousarray(inp["state_gla"][:, 16 * c:16 * (c + 1)])
        m["ss"] = np.ascontiguousarray(inp["state_ssm"][:, 16 * c:16 * (c + 1)])
        m["sc"] = np.ascontiguousarray(inp["state_conv"][:, 16 * c:16 * (c + 1)])
        in_maps.append(m)
    res = run_bass_kernel_spmd(nc, in_maps, core_ids=list(range(8)))
    R = res.results
    y_prompt = np.stack([R[c]["yp"] for c in range(8)], 0)
    y_sample = np.concatenate([R[c]["ys"].reshape(16, 4, D) for c in range(8)], 0)
    gla_p = np.stack([R[c]["gp"] for c in range(8)], 1)
    ssm_p = np.stack([R[c]["spo"] for c in range(8)], 1)
    conv_p = np.stack([R[c]["cpo"] for c in range(8)], 1)
    gla_s = np.concatenate([R[c]["gso"] for c in range(8)], 1)
    ssm_s = np.concatenate([R[c]["sso"] for c in range(8)], 1)
    conv_s = np.concatenate([R[c]["cso"] for c in range(8)], 1)
    return (y_prompt.astype(np.float32), y_sample.astype(np.float32), gla_p.astype(np.float32),
            ssm_p.astype(np.float32), conv_p.astype(np.float32), gla_s.astype(np.float32),
            ssm_s.astype(np.float32), conv_s.astype(np.float32))
```

```python
import contextlib
import numpy as np
import concourse.bass as bass
import concourse.mybir as mybir
from concourse.bass_utils import run_bass_kernel_spmd

F32 = mybir.dt.float32
BF16 = mybir.dt.bfloat16
AF = mybir.ActivationFunctionType
ALU = mybir.AluOpType
AX = mybir.AxisListType

_DS = {F32: 4, BF16: 2}
_MAX_OPS = 10 ** 9
_DEBUG = False
_SKIP = ""
_NOWDMA = False
ENGS = ["pe", "act", "dve", "pool", "sp"]
DMAQ = {"sp": 6, "pool": 10, "act": 2}

D = 1024
FF = 2816
NL = 2
EPS = 1e-6
IN_DIM = 10288
O_Q, O_K, O_V, O_G, O_A, O_Z, O_X, O_DT, O_GA, O_GB = 0, 512, 1024, 2048, 3072, 3088, 5136, 8208, 8240, 9264
NPASS = 4
TP = 512
TS = 64
T = TP + TS


class Prog:
    def __init__(self, nc):
        self.nc = nc
        self.ops = {e: [] for e in ENGS}
        self.recs = {"sb": [], "ps": []}
        self.seen = {e: {} for e in ENGS}
        self.slot_cnt = {(q, i): 0 for q in DMAQ for i in range(DMAQ[q])}
        self.slot_next = {q: 0 for q in DMAQ}
        self.base = {}
        self.sb_off = 16512
        self.nps = 0
        self.uid = 0
        self.nops = 0
        self.max_ops = _MAX_OPS

    def sb(self, name, shape, dtype, at=None):
        row = int(np.prod(shape[1:])) * _DS[dtype]
        if at is None:
            at = self.sb_off
            self.sb_off = (at + row + 63) // 64 * 64
        assert at + row <= 229376, (name, at, row)
        self.uid += 1
        t = self.nc.alloc_sbuf_tensor_at("%s_u%d" % (name, self.uid), list(shape), dtype, offset=at)
        self.base[t.name] = ("sb", at, row)
        return t

    def ps(self, name, shape, dtype):
        t = self.nc.alloc_psum_tensor(name, list(shape), dtype)
        row = int(np.prod(shape[1:])) * _DS[dtype]
        self.base[t.name] = ("ps", self.nps * 2048, row)
        self.nps += (row + 2047) // 2048
        return t

    def box(self, ap):
        name = ap.tensor.name
        if name not in self.base:
            return None
        space, b0, row = self.base[name]
        ds = _DS[ap.dtype]
        rowe = row // ds
        off = ap.offset
        pats = ap.ap
        p_lo = off // rowe
        f_lo = (off % rowe) * ds
        pstep, pcnt = pats[0]
        p_hi = p_lo + (pstep // rowe) * (pcnt - 1) + 1
        ext = sum(abs(s) * (c - 1) for s, c in pats[1:]) + 1
        f_hi = f_lo + ext * ds
        return (space, p_lo, p_hi, b0 + f_lo, b0 + f_hi)

    def _deps(self, eng, reads, writes, ev, skip_same):
        deps = set()
        rb = [b for b in (self.box(a) for a in reads) if b is not None]
        wb = [b for b in (self.box(a) for a in writes) if b is not None]
        if eng == "pe":
            wb = [(b[0], 0, 128, b[3] // 2048 * 2048, (b[4] + 2047) // 2048 * 2048) if b[0] == "ps" else b
                  for b in wb]
        for space in ("sb", "ps"):
            rbs = [b for b in rb if b[0] == space]
            wbs = [b for b in wb if b[0] == space]
            if not rbs and not wbs:
                continue
            keep = []
            for rec in self.recs[space]:
                (_, pl, ph, fl, fh), kind, rev, reng = rec
                hit = False
                covered = False
                for b in wbs:
                    if pl < b[2] and b[1] < ph and fl < b[4] and b[3] < fh:
                        hit = True
                        if b[1] <= pl and ph <= b[2] and b[3] <= fl and fh <= b[4]:
                            covered = True
                if not hit and kind == "w":
                    for b in rbs:
                        if pl < b[2] and b[1] < ph and fl < b[4] and b[3] < fh:
                            hit = True
                            break
                if hit:
                    deps.add(rev)
                if covered:
                    continue
                if kind == "r" and reng == eng and rev[0] == "eng" and ev[0] == "eng":
                    sup = False
                    for b in rbs:
                        if b[1] <= pl and ph <= b[2] and b[3] <= fl and fh <= b[4]:
                            sup = True
                            break
                    if sup:
                        continue
                keep.append(rec)
            for b in rbs:
                keep.append((b, "r", ev, eng))
            for b in wbs:
                keep.append((b, "w", ev, eng))
            self.recs[space] = keep
        seen = self.seen[eng]
        best = {}
        for d in deps:
            if d[0] == "eng":
                _, x, i = d
                if x == eng and (x == "pe" or skip_same):
                    continue
                if seen.get(x, -1) >= i:
                    continue
            else:
                if seen.get(d[1], 0) >= d[2]:
                    continue
            k = d[1]
            if k not in best or best[k][2] < d[2]:
                best[k] = d
        out = []
        for k, d in best.items():
            seen[k] = d[2]
            if d[0] == "eng":
                self.ops[d[1]][d[2]]["needed"] = True
            out.append(d)
        return out

    def op(self, eng, fn, reads=(), writes=()):
        self.nops += 1
        if self.nops > self.max_ops:
            return None
        idx = len(self.ops[eng])
        ev = ("eng", eng, idx)
        waits = self._deps(eng, reads, writes, ev, False)
        self.ops[eng].append({"fn": fn, "waits": waits, "needed": False, "dma": None})
        return ev

    def dma(self, q, out, in_, **kw):
        self.nops += 1
        if self.nops > self.max_ops:
            return None
        k = self.slot_next[q]
        self.slot_next[q] = (k + 1) % DMAQ[q]
        slot = (q, k)
        prev = self.slot_cnt[slot]
        self.slot_cnt[slot] = prev + 1
        ev = ("dma", slot, prev + 1)
        waits = self._deps(q, [in_], [out], ev, False)
        if prev > 0 and self.seen[q].get(slot, 0) < prev:
            waits.append(("dma", slot, prev))
            self.seen[q][slot] = prev
        self.ops[q].append({
            "fn": lambda e, o=out, i=in_, kw=kw: e.dma_start(out=o, in_=i, **kw),
            "waits": waits, "needed": False, "dma": slot})
        return ev

    def mm(self, out, lhsT, rhs, start=True, stop=True):
        return self.op("pe", lambda e: e.matmul(out, lhsT, rhs, start=start, stop=stop),
                       reads=[lhsT, rhs], writes=[out])

    def tr(self, out, in_, ident):
        return self.op("pe", lambda e: e.transpose(out, in_, ident),
                       reads=[in_, ident], writes=[out])

    def act(self, out, in_, func, bias=None, scale=None, accum_out=None):
        kw = {}
        reads = [in_]
        writes = [out]
        if bias is not None:
            kw["bias"] = bias
            if not isinstance(bias, (int, float)):
                reads.append(bias)
        if scale is not None:
            kw["scale"] = scale
            if not isinstance(scale, (int, float)):
                reads.append(scale)
        if accum_out is not None:
            kw["accum_out"] = accum_out
            writes.append(accum_out)
        return self.op("act", lambda e: e.activation(out, in_, func, **kw),
                       reads=reads, writes=writes)

    def tt(self, out, in0, in1, op, eng="dve"):
        return self.op(eng, lambda e: e.tensor_tensor(out, in0, in1, op),
                       reads=[in0, in1], writes=[out])

    def ts(self, out, in0, s1, op0, s2=None, op1=None, eng="dve"):
        reads = [in0] + [s for s in (s1, s2) if s is not None and not isinstance(s, (int, float))]
        if op1 is None:
            f = lambda e: e.tensor_scalar(out, in0, s1, None, op0)
        else:
            f = lambda e: e.tensor_scalar(out, in0, s1, s2, op0, op1)
        return self.op(eng, f, reads=reads, writes=[out])

    def stt(self, out, in0, scalar, in1, op0, op1, eng="dve"):
        reads = [in0, in1] + ([scalar] if not isinstance(scalar, (int, float)) else [])
        return self.op(eng, lambda e: e.scalar_tensor_tensor(out, in0, scalar, in1, op0, op1),
                       reads=reads, writes=[out])

    def copy(self, out, in_, eng="dve"):
        if eng == "act":
            return self.op("act", lambda e: e.activation(out, in_, AF.Copy), reads=[in_], writes=[out])
        return self.op(eng, lambda e: e.tensor_copy(out, in_), reads=[in_], writes=[out])

    def memset(self, ap, val, eng="dve"):
        return self.op(eng, lambda e: e.memset(ap, val), writes=[ap])

    def reduce_sum(self, out, in_, eng="dve"):
        return self.op(eng, lambda e: e.tensor_reduce(out, in_, AX.X, ALU.add), reads=[in_], writes=[out])

    def emit(self):
        nc = self.nc
        with contextlib.ExitStack() as st:
            esem = {e: st.enter_context(nc.semaphore("s_" + e)) for e in ENGS if e != "sp"}
            dsem = {s: st.enter_context(nc.semaphore("d_%s%d" % s)) for s in self.slot_cnt}
            val = {}
            for e in ENGS:
                c = 0
                for i, o in enumerate(self.ops[e]):
                    if o["needed"]:
                        c += 1
                        val[(e, i)] = c
            block = st.enter_context(nc.Block())

            def run(ename, eng):
                for o in self.ops[ename]:
                    for w in o["waits"]:
                        if w[0] == "eng":
                            eng.wait_ge(esem[w[1]], val[(w[1], w[2])])
                        else:
                            eng.wait_ge(dsem[w[1]], 16 * w[2])
                    ins = o["fn"](eng)
                    if o["needed"]:
                        ins.then_inc(esem[ename], 1)
                    if o["dma"] is not None:
                        ins.then_inc(dsem[o["dma"]], 16)
                if ename in DMAQ:
                    for i in range(DMAQ[ename]):
                        c = self.slot_cnt[(ename, i)]
                        if c > 0:
                            eng.wait_ge(dsem[(ename, i)], 16 * c)

            @block.tensor
            def _(e):
                run("pe", e)

            @block.scalar
            def _(e):
                run("act", e)

            @block.vector
            def _(e):
                run("dve", e)

            @block.gpsimd
            def _(e):
                run("pool", e)

            @block.sync
            def _(e):
                run("sp", e)


C_ID, C_CAUS, C_NCAUS, C_NSUF, C_ALLM1 = 0, 128, 256, 384, 512
C_CAUSS, C_NCAUSS, C_NSUFS, C_NSEQ, C_BMCOL, C_ONES, C_BM = 640, 704, 768, 832, 848, 864, 992
NCONST = 2016
NCONST_SB = 992


def make_consts():
    c = np.zeros((128, NCONST), np.float32)
    i = np.arange(128)
    c[:, C_ID:C_ID + 128] = np.eye(128)
    caus = (i[:, None] <= i[None, :]).astype(np.float32)
    c[:, C_CAUS:C_CAUS + 128] = caus
    c[:, C_NCAUS:C_NCAUS + 128] = -caus
    c[:, C_NSUF:C_NSUF + 128] = -(i[:, None] > i[None, :]).astype(np.float32)
    c[:, C_ALLM1:C_ALLM1 + 128] = -1.0
    j = np.arange(64)
    same = (j[:, None] // 4 == j[None, :] // 4)
    cs = ((j[:, None] <= j[None, :]) & same).astype(np.float32)
    c[:64, C_CAUSS:C_CAUSS + 64] = cs
    c[:64, C_NCAUSS:C_NCAUSS + 64] = -cs
    c[:64, C_NSUFS:C_NSUFS + 64] = -((j[:, None] > j[None, :]) & same).astype(np.float32)
    bm = (j[:, None] // 4 == np.arange(16)[None, :]).astype(np.float32)
    c[:64, C_NSEQ:C_NSEQ + 16] = -bm
    c[:64, C_BMCOL:C_BMCOL + 16] = bm
    c[:, C_BM:C_BM + 1024] = np.broadcast_to(bm.T.reshape(1, 1024), (128, 1024))
    c[:, C_ONES:C_ONES + 128] = 1.0
    return c


P_NF1, P_NMIX, P_NF2, P_GLAN, P_SSMN, P_CW, P_CB = 0, 16, 32, 48, 64, 96, 288
P_DTB, P_ALOG, P_DSK, P_W2, P_BGLA = 336, 400, 464, 528, 1552
NPAR = 2576


def make_params(inp):
    p = np.zeros((128, NPAR), np.float32)

    def col(w, nch):
        return np.ascontiguousarray(w.reshape(NL, nch, 128).transpose(2, 0, 1)).reshape(128, NL * nch)

    p[:, P_NF1:P_NF1 + 16] = col(inp["norm_ffn1"], 8)
    p[:, P_NMIX:P_NMIX + 16] = col(inp["norm_mix"], 8)
    p[:, P_NF2:P_NF2 + 16] = col(inp["norm_ffn2"], 8)
    p[:, P_GLAN:P_GLAN + 16] = col(inp["gla_norm"], 8)
    p[:, P_SSMN:P_SSMN + 32] = col(inp["ssm_norm"], 16)
    cw = inp["conv_w"].reshape(NL, 4, 24, 128).transpose(3, 0, 2, 1)
    p[:, P_CW:P_CW + 192] = np.ascontiguousarray(cw).reshape(128, 192)
    p[:, P_CB:P_CB + 48] = col(inp["conv_b"], 24)
    p[:, P_DTB:P_DTB + 64] = np.broadcast_to(inp["dt_bias"].reshape(1, 64), (128, 64))
    p[:, P_ALOG:P_ALOG + 64] = np.broadcast_to(inp["a_log"].reshape(1, 64), (128, 64))
    p[:, P_DSK:P_DSK + 64] = np.broadcast_to(inp["d_skip"].reshape(1, 64), (128, 64))
    p[:16, P_W2:P_W2 + 1024] = inp["w_gla_a2"].transpose(1, 0, 2).reshape(16, 1024)
    p[:, P_BGLA:P_BGLA + 1024] = np.broadcast_to(inp["b_gla_a2"].reshape(1, 1024), (128, 1024))
    return p


def build(last_pass=NPASS, do_mixer=True, stage_limit=3):
    nc = bass.Bass("TRN2", target_bir_lowering=False)
    P = Prog(nc)

    def din(name, shape):
        return nc.dram_tensor(name, list(shape), F32, kind="ExternalInput")

    def dout(name, shape):
        return nc.dram_tensor(name, list(shape), F32, kind="ExternalOutput")

    xp = din("xp", [2048, D])
    xs_in = din("xs", [TS, D])
    sg_in = din("sg", [NL, 16, 4, 128, 256])
    ss_in = din("ss", [NL, 16, 32, 64, 128])
    sc_in = din("sc", [NL, 16, 3, 3072])
    consts_in = din("consts", [128, NCONST])
    params_in = din("params", [128, NPAR])
    nfin_in = din("nfin", [128, D])
    bm_in = din("bm", [128, 1024])
    w_gu = [din("w_ffn1_gu", [NL, D, 2 * FF]), din("w_ffn2_gu", [NL, D, 2 * FF])]
    w_dn = [din("w_ffn1_down", [NL, FF, D]), din("w_ffn2_down", [NL, FF, D])]
    w_in = din("w_in", [NL, D, IN_DIM])
    w_pg = din("w_proj_gla", [NL, D, D])
    w_ps = din("w_proj_ssm", [NL, 2 * D, D])
    w_o = din("w_out", [NL, D, D])

    yp = dout("yp", [2048, D])
    ys = dout("ys", [TS, D])
    gp = dout("gp", [NL, 4, 128, 256])
    spo = dout("spo", [NL, 32, 64, 128])
    cpo = dout("cpo", [NL, 3, 3072])
    gso = dout("gso", [NL, 16, 4, 128, 256])
    sso = dout("sso", [NL, 16, 32, 64, 128])
    cso = dout("cso", [NL, 16, 3, 3072])
    dbg = dout("dbg", [128, 8, 512]) if _DEBUG else None
    dbg_done = set()

    def dump(i, ap, np_=128, nf=512):
        if dbg is None or i in dbg_done:
            return
        dbg_done.add(i)
        P.dma("sp", dbg[:np_, i, :nf], ap)

    CON = P.sb("CON", [128, NCONST_SB], F32)
    BMt = P.sb("BMt", [128, 16, 64], BF16)
    PAR = P.sb("PAR", [128, NPAR], F32)
    identb = P.sb("identb", [128, 128], BF16)
    x = P.sb("x", [128, 5, D], F32)
    xnT = P.sb("xnT", [128, 8, T], BF16)
    NSLOT = 3
    SLOTE = 4096
    wring = P.sb("wring", [128, NSLOT, SLOTE], BF16)
    Sst = P.sb("Sst", [128, NL, 1024], F32)
    Sbf = P.sb("Sbf", [128, 1024], BF16)
    hT = P.sb("hT", [128, NL, 2048], F32)
    hTbf = P.sb("hTbf", [128, 2048], BF16)
    halo = P.sb("halo", [128, NL, 24, 3], BF16)
    xsb = P.sb("xsb", [128, D], BF16)
    junk = P.sb("junk", [128, D], BF16)
    st1 = P.sb("st1", [128, 8], F32)
    st2 = P.sb("st2", [128, 8], F32)
    expA = P.sb("expA", [128, 64], F32)
    onesb = P.sb("onesb", [128, 128], BF16)
    hl = P.sb("hl", [128, 4], BF16)
    hlf = P.sb("hlf", [128, 4], F32)
    rrow = P.sb("rrow", [128, 256], BF16)
    A0 = P.sb_off

    def cv(off, n, rows=128):
        return CON[:rows, off:off + n]

    identf = cv(C_ID, 128)
    ones_row = CON[0:1, C_ONES:C_ONES + 128]

    def par(off, n, rows=128):
        return PAR[:rows, off:off + n]

    PP = [P.ps("pp%d" % i, [128, 1024], F32) for i in range(4)]
    bstate = {"b": 0}
    reserved = set()

    def bank():
        while True:
            k = bstate["b"]
            bstate["b"] = (k + 1) % 8
            if k // 2 not in reserved:
                break
        return PP[k // 2][:, (k % 2) * 512:(k % 2 + 1) * 512]

    def bank2(reserve=False):
        while True:
            k = (bstate["b"] + 1) // 2 % 4
            bstate["b"] = (2 * k + 2) % 8
            if k not in reserved:
                break
        if reserve:
            reserved.add(k)
        return PP[k]

    def bank_r():
        b = bank()
        k = (bstate["b"] - 1) % 8 // 2
        reserved.add(k)
        return b

    def bf(ap):
        return ap.bitcast(BF16)

    plan = []

    def wview(w2d, kc, c0, ncols, k0=0):
        return w2d.rearrange("(kc kp) n -> kp kc n", kp=128)[:, k0:k0 + kc, c0:c0 + ncols]

    def plan_ffn(l, which):
        for j in range(11):
            plan.append(("gu", [(wview(w_gu[which][l], 8, 256 * j, 256), 8, 256),
                                (wview(w_gu[which][l], 8, FF + 256 * j, 256), 8, 256)]))
        for h in range(2):
            for u in range(3):
                nk = 8 if u < 2 else 6
                plan.append(("dn", [(wview(w_dn[which][l], nk, 512 * h, 512, k0=8 * u), nk, 512)]))

    def plan_mixer(l):
        wi = w_in[l]
        plan.append(("q", [(wview(wi, 8, O_Q, 512), 8, 512)]))
        plan.append(("k", [(wview(wi, 8, O_K, 512), 8, 512)]))
        for u in range(2):
            plan.append(("v", [(wview(wi, 8, O_V + 512 * u, 512), 8, 512)]))
        for u in range(2):
            plan.append(("g", [(wview(wi, 8, O_G + 512 * u, 512), 8, 512)]))
        plan.append(("a", [(wview(wi, 8, O_A - 112, 128), 8, 128)]))
        if stage_limit < 2:
            return
        plan.append(("dt", [(wview(wi, 8, O_DT - 96, 128), 8, 128)]))
        for g in range(4):
            plan.append(("z", [(wview(wi, 8, O_Z + 512 * g, 512), 8, 512)]))
            plan.append(("xs", [(wview(wi, 8, O_X + 512 * g, 512), 8, 512)]))
            plan.append(("bc", [(wview(wi, 8, O_X + 2048 + 128 * g, 128), 8, 128),
                                (wview(wi, 8, O_X + 2560 + 128 * g, 128), 8, 128)]))
        if stage_limit < 3:
            return
        for u in range(2):
            plan.append(("ga", [(wview(wi, 8, O_GA + 512 * u, 512), 8, 512)]))
        for u in range(2):
            plan.append(("gb", [(wview(wi, 8, O_GB + 512 * u, 512), 8, 512)]))
        for u in range(2):
            plan.append(("pg", [(wview(w_pg[l], 8, 512 * u, 512), 8, 512)]))
        for u in range(4):
            plan.append(("ps", [(wview(w_ps[l], 16, 256 * u, 256), 16, 256)]))
        for u in range(2):
            plan.append(("wo", [(wview(w_o[l], 8, 512 * u, 512), 8, 512)]))

    for p_ in range(last_pass):
        for l in range(NL):
            plan_ffn(l, 0)
            if do_mixer:
                plan_mixer(l)
            plan_ffn(l, 1)

    wst = {"issued": 0, "next": 0}

    def w_issue(i):
        if _NOWDMA:
            return
        tag, parts = plan[i]
        s = i % NSLOT
        off = 0
        for (src, kc, ncols) in parts:
            dst = wring[:, s, off:off + kc * ncols].rearrange("p (k n) -> p k n", k=kc)
            if kc > 8:
                P.dma("pool", dst[:, 0:8, :], src[:, 0:8, :])
                P.dma("pool", dst[:, 8:kc, :], src[:, 8:kc, :])
            else:
                P.dma("pool", dst, src)
            off += kc * ncols

    def wnext(tag):
        i = wst["next"]
        assert plan[i][0] == tag, (plan[i][0], tag)
        while wst["issued"] < min(len(plan), i + NSLOT):
            w_issue(wst["issued"])
            wst["issued"] += 1
        wst["next"] = i + 1
        s = i % NSLOT
        views = []
        off = 0
        for (_, kc, ncols) in plan[i][1]:
            views.append(wring[:, s, off:off + kc * ncols].rearrange("p (k n) -> p k n", k=kc))
            off += kc * ncols
        return views

    P.dma("sp", CON[:, :], consts_in[:, 0:NCONST_SB])
    P.dma("pool", BMt[:, :, :], bm_in.ap().rearrange("p (b t) -> p b t", b=16) if hasattr(bm_in, "ap") else bm_in[:, :].rearrange("p (b t) -> p b t", b=16))
    P.dma("sp", PAR[:, :], params_in[:, :])
    P.copy(identb[:, :], identf)
    P.memset(onesb[:, :], 1.0)
    P.memset(Sst[:, :, :], 0.0)
    P.memset(hT[:, :, :], 0.0)
    P.memset(halo[:, :, :, :], 0.0)
    P.act(expA[:, :], par(P_ALOG, 64), AF.Exp)

    def rms_T(tiles, ncol_off, l):
        ncol = PAR[:, ncol_off + 8 * l: ncol_off + 8 * l + 8]
        for (kind, n, c0, ti) in tiles:
            xt = x[:n, ti, :]
            P.act(junk[:n, :], xt, AF.Square, accum_out=st1[:n, 0:1])
            P.ts(st2[:n, 0:1], st1[:n, 0:1], 1.0 / D, ALU.mult, EPS, ALU.add)
            P.act(st2[:n, 0:1], st2[:n, 0:1], AF.Ln); P.act(st2[:n, 0:1], st2[:n, 0:1], AF.Exp, scale=-0.5)
            P.act(xsb[:n, :], xt, AF.Copy, scale=st2[:n, 0:1])
            pt = bf(bank()).rearrange("p (c n) -> p c n", c=8)
            for c in range(8):
                P.tr(pt[:, c, :n], xsb[:n, c * 128:(c + 1) * 128], identb[:n, :n])
            P.tt(xnT[:, :, c0:c0 + n], pt[:, :, :n], ncol.unsqueeze(2).to_broadcast([128, 8, n]), ALU.mult)

    def ffn(tiles, blocks, l, which):
        rms_T(tiles, P_NF1 if which == 0 else P_NF2, l)
        actT = P.sb("actT", [128, 22, T], BF16, at=A0)
        sgt = P.sb("sgt", [128, 2, 512], F32, at=A0 + 22 * T * 2)
        k = 0
        for j in range(11):
            wg, wu = wnext("gu")
            for fc in range(2):
                f = 2 * j + fc
                for (c0, c1) in blocks:
                    n = c1 - c0
                    pg = bank()
                    pu = bank()
                    for kc in range(8):
                        P.mm(pg[:, :n], wg[:, kc, fc * 128:(fc + 1) * 128], xnT[:, kc, c0:c1],
                             start=(kc == 0), stop=(kc == 7))
                    for kc in range(8):
                        P.mm(pu[:, :n], wu[:, kc, fc * 128:(fc + 1) * 128], xnT[:, kc, c0:c1],
                             start=(kc == 0), stop=(kc == 7))
                    s = sgt[:, k % 2, :n]
                    k += 1
                    P.act(s, pg[:, :n], AF.Silu)
                    P.tt(actT[:, f, c0:c1], s, pu[:, :n], ALU.mult)
        for h in range(2):
            po = [bank() for _ in tiles]
            for u in range(3):
                (wd,) = wnext("dn")
                nk = 8 if u < 2 else 6
                for i, (kind, n, c0, ti) in enumerate(tiles):
                    for kk in range(nk):
                        P.mm(po[i][:n, :], actT[:, 8 * u + kk, c0:c0 + n], wd[:, kk, :],
                             start=(u == 0 and kk == 0), stop=(u == 2 and kk == nk - 1))
            for i, (kind, n, c0, ti) in enumerate(tiles):
                xv = x[:n, ti, 512 * h:512 * (h + 1)]
                P.stt(xv, po[i][:n, :], 0.5, xv, ALU.mult, ALU.add)

    def mixer(tiles, blocks, l, pi):
        rms_T(tiles, P_NMIX, l)
        last = (pi == NPASS - 1)
        ogT = P.sb("ogT", [128, 8, T], BF16, at=A0)
        ynT = P.sb("ynT", [128, 16, T], BF16, at=A0 + 8 * T * 2)
        rstdbc = P.sb("rstdbc", [128, T], F32, at=A0 + 24 * T * 2)
        G0 = A0 + 24 * T * 2 + T * 4
        ar = {"o": G0}

        def tmp(name, shape, dtype):
            t = P.sb(name, shape, dtype, at=ar["o"])
            row = int(np.prod(shape[1:])) * _DS[dtype]
            ar["o"] = (ar["o"] + row + 63) // 64 * 64
            return t

        def proj_fm(w, ncol_chunks, evac):
            for c in range(ncol_chunks):
                for (c0, c1) in blocks:
                    n = c1 - c0
                    pb = bank()
                    for kc in range(8):
                        P.mm(pb[:, :n], w[:, kc, c * 128:(c + 1) * 128], xnT[:, kc, c0:c1],
                             start=(kc == 0), stop=(kc == 7))
                    evac(c, c0, c1, pb[:, :n])

        def proj_tm(w, ncols, evac):
            for tl in tiles:
                (kind, n, c0, ti) = tl
                pb = bank()
                for kc in range(8):
                    P.mm(pb[:n, :ncols], xnT[:, kc, c0:c0 + n], w[:, kc, :ncols],
                         start=(kc == 0), stop=(kc == 7))
                evac(tl, pb[:n, :ncols])

        qT = tmp("qT", [128, 4, T], BF16)
        kT = tmp("kT", [128, 4, T], BF16)
        ktok = tmp("ktok", [128, 5, 512], BF16)
        vtok = tmp("vtok", [128, 5, 1024], BF16)
        sgl = tmp("sgl", [128, 5, 1024], BF16)
        alrT = tmp("alrT", [128, T], F32)
        e_ = tmp("e_", [128, 512], F32)
        eb = tmp("eb", [128, 4, 128], F32)
        enb = tmp("enb", [128, 4, 128], F32)
        esuf = tmp("esuf", [128, 512], F32)
        qin = tmp("qin", [128, 4, 128], BF16)
        kin = tmp("kin", [128, 4, 128], BF16)
        kout = tmp("kout", [128, 512], BF16)
        scm = tmp("scm", [128, 4, 128], BF16)
        og = tmp("og", [128, 1024], BF16)
        has_s = any(t[0] == "s" for t in tiles)
        if has_s:
            otmp = tmp("otmp", [128, 1024], F32)
            qmb = tmp("qmb", [128, 2, 4, 64], F32)
            qinf = tmp("qinf", [128, 4, 64], F32)
            koutm = tmp("koutm", [128, 2, 512], BF16)
            S0 = tmp("S0", [128, 1024], F32)
            Sout = tmp("Sout", [128, 1024], F32)

        (wq,) = wnext("q")
        proj_fm(wq, 4, lambda c, c0, c1, ps: P.copy(qT[:, c, c0:c1], ps, eng="act"))
        (wk,) = wnext("k")
        proj_fm(wk, 4, lambda c, c0, c1, ps: P.copy(kT[:, c, c0:c1], ps, eng="act"))
        proj_tm(wk, 512, lambda tl, ps: P.copy(ktok[:tl[1], tl[3], :], ps))
        for u in range(2):
            (wv,) = wnext("v")
            proj_tm(wv, 512, lambda tl, ps, u=u: P.copy(vtok[:tl[1], tl[3], 512 * u:512 * (u + 1)], ps))
        for u in range(2):
            (wg_,) = wnext("g")
            proj_tm(wg_, 512, lambda tl, ps, u=u: P.act(sgl[:tl[1], tl[3], 512 * u:512 * (u + 1)], ps, AF.Silu))
        (wa,) = wnext("a")
        for (c0, c1) in blocks:
            n = c1 - c0
            pb = bank()
            for kc in range(8):
                P.mm(pb[:16, :n], wa[:, kc, 112:128], xnT[:, kc, c0:c1], start=(kc == 0), stop=(kc == 7))
            P.copy(alrT[:16, c0:c1], pb[:16, :n])

        w2 = PAR[:16, P_W2 + 512 * l:P_W2 + 512 * (l + 1)]
        bgla = PAR[:, P_BGLA + 512 * l:P_BGLA + 512 * (l + 1)]
        glan = PAR[:, P_GLAN + 8 * l:P_GLAN + 8 * l + 8]

        P.copy(Sbf[:, :], Sst[:, l, :], eng="act")
        for (kind, n, c0, ti) in tiles:
            cs = slice(c0, c0 + n)
            if kind == "p":
                CAUS, NCAUS, NSUF = cv(C_CAUS, n), cv(C_NCAUS, n), cv(C_NSUF, n)
            else:
                CAUS, NCAUS, NSUF = cv(C_CAUSS, n, 64), cv(C_NCAUSS, n, 64), cv(C_NSUFS, n, 64)
            pb = bank()
            P.mm(pb[:n, :], alrT[:16, cs], w2, start=True, stop=True)
            P.tt(e_[:n, :], pb[:n, :], bgla[:n, :], ALU.add)
            P.act(e_[:n, :], e_[:n, :], AF.Exp, scale=-1.0)
            P.act(e_[:n, :], e_[:n, :], AF.Ln, bias=1.0)
            P.ts(e_[:n, :], e_[:n, :], 1.0 / 16.0, ALU.mult)
            pc = bank().rearrange("p (h n) -> p h n", h=4)
            for h in range(4):
                P.mm(pc[:, h, :n], e_[:n, h * 128:(h + 1) * 128], NCAUS)
            psf = bank()
            P.mm(psf[:n, :], NSUF, e_[:n, :])
            P.act(eb[:, :, :n], pc[:, :, :n], AF.Exp)
            P.act(enb[:, :, :n], pc[:, :, :n], AF.Exp, scale=-1.0)
            P.act(esuf[:n, :], psf[:n, :], AF.Exp)
            P.stt(qin[:, :, :n], qT[:, :, cs], 128.0 ** -0.5, eb[:, :, :n], ALU.mult, ALU.mult)
            P.tt(kin[:, :, :n], kT[:, :, cs], enb[:, :, :n], ALU.mult)
            P.tt(kout[:n, :], ktok[:n, ti, :], esuf[:n, :], ALU.mult)
            psc = bank().rearrange("p (h n) -> p h n", h=4)
            for h in range(4):
                P.mm(psc[:n, h, :n], kin[:, h, :n], qin[:, h, :n])
            P.tt(scm[:n, :, :n], psc[:n, :, :n], CAUS.unsqueeze(1).to_broadcast([n, 4, n]), ALU.mult)
            if kind == "p":
                po = bank2()
                for h in range(4):
                    hs = slice(h * 256, (h + 1) * 256)
                    P.mm(po[:n, hs], scm[:n, h, :n], vtok[:n, ti, hs], start=True, stop=False)
                    P.mm(po[:n, hs], qin[:, h, :n], Sbf[:, hs], start=False, stop=True)
                pds = bank2()
                for h in range(4):
                    hs = slice(h * 256, (h + 1) * 256)
                    P.mm(pds[:, hs], kout[:n, h * 128:(h + 1) * 128], vtok[:n, ti, hs])
                for h in range(4):
                    hs = slice(h * 256, (h + 1) * 256)
                    P.stt(Sst[:, l, hs], Sst[:, l, hs], eb[:, h, n - 1:n], pds[:, hs], ALU.mult, ALU.add)
                P.copy(Sbf[:, :], Sst[:, l, :], eng="act")
                osrc = po
            else:
                po = bank2(reserve=True)
                pA = bank2(reserve=True)
                pB = bank2(reserve=True)
                pih = [pA[:, 0:256], pA[:, 512:768], pB[:, 0:256], pB[:, 512:768]]
                P.stt(qinf[:, :, :n], qT[:, :, cs], 128.0 ** -0.5, eb[:, :, :n], ALU.mult, ALU.mult)
                for h in range(4):
                    hs = slice(h * 256, (h + 1) * 256)
                    P.mm(po[:n, hs], scm[:n, h, :n], vtok[:n, ti, hs], start=True, stop=True)
                for b in range(16):
                    s0 = S0[:, :]
                    P.dma("sp", s0.rearrange("p (h v) -> p h v", h=4), sg_in[l, b].rearrange("h d v -> d h v"))
                    qm = qmb[:, b % 2, :, :]
                    P.tt(qm, qinf[:, :, :], BMt[:, b, :].unsqueeze(1).to_broadcast([128, 4, 64]), ALU.mult)
                    for h in range(4):
                        hs = slice(h * 256, (h + 1) * 256)
                        P.mm(pih[h][:n, :], qm[:, h, :], s0[:, hs], start=(b == 0), stop=(b == 15))
                    km = koutm[:n, b % 2, :]
                    P.ts(km, kout[:n, :], CON[:n, C_BMCOL + b:C_BMCOL + b + 1], ALU.mult)
                    pds = bank2()
                    so = Sout[:, :]
                    for h in range(4):
                        hs = slice(h * 256, (h + 1) * 256)
                        P.mm(pds[:, hs], km[:, h * 128:(h + 1) * 128], vtok[:n, ti, hs])
                    for h in range(4):
                        hs = slice(h * 256, (h + 1) * 256)
                        P.stt(so[:, hs], s0[:, hs], eb[:, h, 4 * b + 3:4 * b + 4], pds[:, hs], ALU.mult, ALU.add)
                    P.dma("sp", gso[l, b].rearrange("h d v -> d h v"), so.rearrange("p (h v) -> p h v", h=4))
                for h in range(4):
                    P.copy(otmp[:n, h * 256:(h + 1) * 256], pih[h][:n, :], eng="act")
                dump(4, otmp[:n, 0:512], 64, 512)
                P.tt(otmp[:n, :], otmp[:n, :], po[:n, :], ALU.add)
                dump(0, otmp[:n, 0:512], 64, 512)
                dump(1, otmp[:n, 512:1024], 64, 512)
                reserved.clear()
                osrc = otmp
            for h in range(4):
                P.act(junk[:n, :256], osrc[:n, h * 256:(h + 1) * 256], AF.Square, accum_out=st1[:n, h:h + 1])
            P.ts(st2[:n, 0:4], st1[:n, 0:4], 1.0 / 256.0, ALU.mult, EPS, ALU.add)
            P.act(st2[:n, 0:4], st2[:n, 0:4], AF.Ln); P.act(st2[:n, 0:4], st2[:n, 0:4], AF.Exp, scale=-0.5)
            for h in range(4):
                hs = slice(h * 256, (h + 1) * 256)
                P.stt(og[:n, hs], osrc[:n, hs], st2[:n, h:h + 1], sgl[:n, ti, hs], ALU.mult, ALU.mult)
            pt = bf(bank()).rearrange("p (c n) -> p c n", c=8)
            for c in range(8):
                P.tr(pt[:, c, :n], og[:n, c * 128:(c + 1) * 128], identb[:n, :n])
            P.tt(ogT[:, :, cs], pt[:, :, :n], glan.unsqueeze(2).to_broadcast([128, 8, n]), ALU.mult)

        if last:
            P.dma("sp", gp[l].rearrange("h d v -> d h v"), Sst[:, l, :].rearrange("p (h v) -> p h v", h=4))

        if stage_limit < 2:
            return
        ar["o"] = G0
        dtall = tmp("dtall", [128, 5, 32], F32)
        rall = tmp("rall", [128, 5, 32], F32)
        eaall = tmp("eaall", [128, 5, 32], F32)
        esall = tmp("esall", [128, 5, 32], F32)
        decall = tmp("decall", [128, 5, 32], F32)
        ssq = tmp("ssq", [128, 5, 4], F32)
        szg = tmp("szg", [128, 5, 512], BF16)
        xcT = tmp("xcT", [128, 6, T], BF16)
        pre = tmp("pre", [128, 2, 3 + TP], BF16)
        acc = tmp("acc", [128, 512], F32)
        rbig = tmp("rbig", [128, 8, 128], F32)
        seg = tmp("seg", [128, 8, 128], F32)
        MT = tmp("MT", [128, 8, 128], BF16)
        cbm = tmp("cbm", [128, 128], F32)
        xstok = tmp("xstok", [128, 512], BF16)
        Btok = tmp("Btok", [128, 128], BF16)
        xg = tmp("xg", [128, 512], BF16)
        xgw = tmp("xgw", [128, 512], BF16)
        t1 = tmp("t1", [128, 512], F32)
        t2 = tmp("t2", [128, 512], F32)
        yz = tmp("yz", [128, 512], BF16)
        cvt = tmp("cvt", [128, 2, 512], F32)
        if has_s:
            ext = tmp("ext", [128, 2, 16, 7], BF16)
            accs = tmp("accs", [128, 16, 4], F32)
            sctok = tmp("sctok", [128, 768], F32)
            scT = tmp("scT", [128, 6, 16, 3], BF16)
            sctokb = tmp("sctokb", [128, 768], BF16)
            CTm = tmp("CTm", [128, 16, 64], BF16)
            h0b = tmp("h0b", [128, 2, 512], BF16)
            rrep = tmp("rrep", [128, 8, 64], F32)
            dcol = tmp("dcol", [128, 4, 16], F32)
            xgwm = tmp("xgwm", [128, 2, 512], BF16)
            h0 = tmp("h0", [128, 2, 4, 128], F32)
            h0T = tmp("h0T", [128, 2, 512], BF16)
            hout = tmp("hout", [128, 2, 4, 128], F32)
        assert ar["o"] <= 229376, ar["o"]

        dtb = PAR[:, P_DTB + 32 * l:P_DTB + 32 * (l + 1)]
        eA = expA[:, 32 * l:32 * (l + 1)]
        dsk = PAR[:, P_DSK + 32 * l:P_DSK + 32 * (l + 1)]
        ssmn = PAR[:, P_SSMN + 16 * l:P_SSMN + 16 * (l + 1)]
        cw = PAR[:, P_CW + 96 * l:P_CW + 96 * (l + 1)].rearrange("p (c i) -> p c i", i=4)
        cb = PAR[:, P_CB + 24 * l:P_CB + 24 * (l + 1)]

        P.copy(hTbf[:, :], hT[:, l, :], eng="act")
        (wdt,) = wnext("dt")

        def dt_evac(tl, ps):
            (kind, n, c0, ti) = tl
            if kind == "p":
                NCAUS, NSUF = cv(C_NCAUS, n), cv(C_NSUF, n)
            else:
                NCAUS, NSUF = cv(C_NCAUSS, n, 64), cv(C_NSUFS, n, 64)
            d = dtall[:n, ti, :]
            P.tt(d, ps, dtb[:n, :], ALU.add)
            P.act(d, d, AF.Exp)
            P.act(d, d, AF.Ln, bias=1.0)
            r = rall[:n, ti, :]
            P.tt(r, d, eA[:n, :], ALU.mult)
            p1 = bank()
            P.mm(p1[:n, 0:32], NCAUS, r)
            P.act(eaall[:n, ti, :], p1[:n, 0:32], AF.Exp)
            p2 = bank()
            P.mm(p2[:n, 0:32], NSUF, r)
            P.act(esall[:n, ti, :], p2[:n, 0:32], AF.Exp)
            if kind == "p":
                p3 = bank()
                P.mm(p3[:, 0:32], cv(C_ALLM1, 128), r)
                P.act(decall[:, ti, :], p3[:, 0:32], AF.Exp)

        proj_tm(wdt[:, :, 96:128], 32, dt_evac)

        def conv_chunk(j, ch, c_p, c_s, k):
            if c_p is not None:
                pr = pre[:, k % 2, :]
                P.copy(pr[:, 0:3], halo[:, l, ch, :])
                P.copy(pr[:, 3:3 + TP], c_p, eng="act")
                P.ts(acc[:, :], pr[:, 3:3 + TP], cw[:, ch, 3:4], ALU.mult, cb[:, ch:ch + 1], ALU.add)
                for i in range(3):
                    P.stt(acc[:, :], pr[:, i:i + TP], cw[:, ch, i:i + 1], acc[:, :], ALU.mult, ALU.add)
                P.act(xcT[:, j, 0:TP], acc[:, :], AF.Silu)
                P.copy(halo[:, l, ch, :], pr[:, TP:TP + 3])
            if c_s is not None:
                ex = ext[:, k % 2, :, :]
                P.copy(ex[:, :, 0:3], scT[:, j, :, :])
                P.copy(ex[:, :, 3:7], c_s.rearrange("p (b t) -> p b t", b=16), eng="act")
                P.ts(accs[:, :, :], ex[:, :, 3:7], cw[:, ch, 3:4], ALU.mult, cb[:, ch:ch + 1], ALU.add)
                for i in range(3):
                    P.stt(accs[:, :, :], ex[:, :, i:i + 4], cw[:, ch, i:i + 1], accs[:, :, :], ALU.mult, ALU.add)
                P.act(xcT[:, j, TP:T].rearrange("p (b t) -> p b t", b=16), accs[:, :, :], AF.Silu)

        kconv = [0]
        ncv = [0]

        def conv_unit(w, nch, jbase, chs, colbase):
            for c in range(nch):
                c_p = c_s = None
                for (c0, c1) in blocks:
                    n = c1 - c0
                    pb = bank()
                    for kc in range(8):
                        P.mm(pb[:, :n], w[:, kc, c * 128:(c + 1) * 128], xnT[:, kc, c0:c1],
                             start=(kc == 0), stop=(kc == 7))
                    if c0 == 0:
                        c_p = pb[:, :TP]
                    else:
                        c_s = pb[:, :TS]
                conv_chunk(jbase + c, chs[c], c_p, c_s, kconv[0])
                kconv[0] += 1
            ncols = nch * 128
            if has_s:
                pb = bank()
                for kc in range(8):
                    P.mm(pb[:TS, :ncols], xnT[:, kc, TP:T], w[:, kc, :ncols], start=(kc == 0), stop=(kc == 7))
                cvb = cvt[:, ncv[0] % 2, :]
                ncv[0] += 1
                P.copy(cvb[:TS, :ncols], pb[:TS, :ncols])
                for jj in range(3):
                    P.dma("sp", cso[l, :, jj, colbase:colbase + ncols], cvb[1 + jj:TS:4, :ncols])
            if last:
                pb = bank()
                for kc in range(8):
                    P.mm(pb[:32, :ncols], xnT[:, kc, TP - 32:TP], w[:, kc, :ncols], start=(kc == 0), stop=(kc == 7))
                cvb = cvt[:, ncv[0] % 2, :]
                ncv[0] += 1
                P.copy(cvb[:32, :ncols], pb[:32, :ncols])
                P.dma("sp", cpo[l, :, colbase:colbase + ncols], cvb[29:32, :ncols])

        for g in range(4):
            gh = slice(8 * g, 8 * g + 8)
            (wz,) = wnext("z")
            proj_tm(wz, 512, lambda tl, ps: P.act(szg[:tl[1], tl[3], :], ps, AF.Silu))
            if has_s:
                scv = sc_in[l].rearrange("b j c -> (b j) c")
                P.dma("sp", sctok[:48, 0:512], scv[:, 512 * g:512 * (g + 1)])
                P.dma("sp", sctok[:48, 512:640], scv[:, 2048 + 128 * g:2048 + 128 * (g + 1)])
                P.dma("sp", sctok[:48, 640:768], scv[:, 2560 + 128 * g:2560 + 128 * (g + 1)])
                P.copy(sctokb[:48, :], sctok[:48, :])
                pt = bf(bank()).rearrange("p (c n) -> p c n", c=8)
                for c in range(6):
                    P.tr(pt[:, c, :48], sctokb[:48, c * 128:(c + 1) * 128], identb[:48, :48])
                P.copy(scT[:, :, :, :].rearrange("p c b j -> p c (b j)"), pt[:, 0:6, :48])
            (wx,) = wnext("xs")
            conv_unit(wx, 4, 0, [4 * g + c for c in range(4)], 512 * g)
            wb_, wc_ = wnext("bc")
            conv_unit(wb_, 1, 4, [16 + g], 2048 + 128 * g)
            conv_unit(wc_, 1, 5, [20 + g], 2560 + 128 * g)

            for (kind, n, c0, ti) in tiles:
                cs = slice(c0, c0 + n)
                if kind == "p":
                    CAUS, NSUF = cv(C_CAUS, n), cv(C_NSUF, n)
                else:
                    CAUS, NSUF = cv(C_CAUSS, n, 64), cv(C_NSUFS, n, 64)
                pcb = bank()
                P.mm(pcb[:n, :n], xcT[:, 4, cs], xcT[:, 5, cs])
                P.tt(cbm[:n, :n], pcb[:n, :n], CAUS, ALU.mult)
                P.tt(rbig[:n, :, :n], rall[:n, ti, gh].unsqueeze(2).to_broadcast([n, 8, n]),
                     CAUS.unsqueeze(1).to_broadcast([n, 8, n]), ALU.mult)
                pD = bank2().rearrange("p (h n) -> p h n", h=8)
                if n == 128:
                    for hf in range(2):
                        P.mm(pD[:n, 4 * hf:4 * hf + 4, :n], NSUF, rbig[:n, 4 * hf:4 * hf + 4, :n])
                else:
                    for hh in range(8):
                        P.mm(pD[:n, hh, :n], NSUF, rbig[:n, hh, :n])
                P.act(seg[:n, :, :n], pD[:n, :, :n], AF.Exp)
                P.tt(MT[:n, :, :n], seg[:n, :, :n], cbm[:n, :n].unsqueeze(1).to_broadcast([n, 8, n]), ALU.mult)
                ptb = bf(bank()).rearrange("p (c n) -> p c n", c=8)
                for j in range(5):
                    P.tr(ptb[:n, j, :], xcT[:, j, cs], identb[:, :])
                P.copy(xstok[:n, :].rearrange("p (c n) -> p c n", c=4), ptb[:n, 0:4, :], eng="act")
                P.copy(Btok[:n, :], ptb[:n, 4, :])
                xs3 = xstok[:n, :].rearrange("p (h q) -> p h q", h=8)
                P.tt(xg[:n, :].rearrange("p (h q) -> p h q", h=8), xs3,
                     dtall[:n, ti, gh].unsqueeze(2).to_broadcast([n, 8, 64]), ALU.mult)
                P.tt(xgw[:n, :].rearrange("p (h q) -> p h q", h=8), xg[:n, :].rearrange("p (h q) -> p h q", h=8),
                     esall[:n, ti, gh].unsqueeze(2).to_broadcast([n, 8, 64]), ALU.mult)
                pyd = bank_r() if kind == "s" else bank()
                for hh in range(8):
                    P.mm(pyd[:n, hh * 64:(hh + 1) * 64], MT[:n, hh, :n], xg[:n, hh * 64:(hh + 1) * 64])
                pyo = bank_r() if kind == "s" else bank()
                if kind == "p":
                    P.mm(pyo[:n, :], xcT[:, 5, cs], hTbf[:, 512 * g:512 * (g + 1)])
                else:
                    BM = BMt[:, :, :]
                    P.tt(CTm[:, :, :], xcT[:, 5, cs].unsqueeze(1).to_broadcast([128, 16, 64]), BM, ALU.mult)
                    P.memset(Btok[64:128, :], 0.0)
                    P.memset(xgwm[64:128, :, :], 0.0)
                    P.copy(rrep[:n, :, :], rall[:n, ti, gh].unsqueeze(2).to_broadcast([n, 8, 64]))
                    pdc = bank().rearrange("p (j b) -> p j b", j=4)
                    for j in range(4):
                        P.mm(pdc[:, j, 0:16], rrep[:n, 2 * j:2 * j + 2, :].rearrange("p h q -> p (h q)"),
                             CON[:n, C_NSEQ:C_NSEQ + 16])
                    P.act(dcol[:, :, :], pdc[:, :, 0:16], AF.Exp)
                    h0src = lambda b: ss_in[l, b, gh].rearrange("h q d -> (h q) d").rearrange("(j p) d -> p j d", p=128)
                    pT = bf(bank()).rearrange("p (j n) -> p j n", j=8)
                    pds = bank().rearrange("p (j n) -> p j n", j=4)
                    for b in range(16):
                        hb = h0[:, b % 2, :, :]
                        P.dma("sp", hb, h0src(b))
                        xm = xgwm[:n, b % 2, :]
                        P.ts(xm, xgw[:n, :], CON[:n, C_BMCOL + b:C_BMCOL + b + 1], ALU.mult)
                        ho = hout[:, b % 2, :, :]
                        for j in range(4):
                            P.mm(pds[:, j, :], xgwm[:, b % 2, j * 128:(j + 1) * 128], Btok[:, :])
                        for j in range(4):
                            P.stt(ho[:, j, :], hb[:, j, :], dcol[:, j, b:b + 1], pds[:, j, :], ALU.mult, ALU.add)
                        P.dma("sp", sso[l, b, gh].rearrange("h q d -> (h q) d").rearrange("(j p) d -> p j d", p=128), ho)
                    for b in range(16):
                        hb = h0[:, b % 2, :, :]
                        P.dma("sp", hb, h0src(b))
                        hbT = h0T[:, b % 2, :]
                        hbb = h0b[:, b % 2, :]
                        P.copy(hbb, hb.rearrange("p j n -> p (j n)"), eng="act")
                        for j in range(4):
                            P.tr(pT[:, j, :], hbb[:, j * 128:(j + 1) * 128], identb[:, :])
                        P.copy(hbT.rearrange("p (j n) -> p j n", j=4), pT[:, 0:4, :], eng="act")
                        P.mm(pyo[:n, :], CTm[:, b, :], hbT, start=(b == 0), stop=(b == 15))
                reserved.clear()
                t13 = t1[:n, :].rearrange("p (h q) -> p h q", h=8)
                P.tt(t13, pyo[:n, :].rearrange("p (h q) -> p h q", h=8),
                     eaall[:n, ti, gh].unsqueeze(2).to_broadcast([n, 8, 64]), ALU.mult)
                if kind == "s":
                    dump(3, t1[:n, :], 64, 512)
                P.tt(t1[:n, :], t1[:n, :], pyd[:n, :], ALU.add)
                if kind == "s":
                    dump(2, t1[:n, :], 64, 512)
                P.tt(t2[:n, :].rearrange("p (h q) -> p h q", h=8), xs3,
                     dsk[:n, gh].unsqueeze(2).to_broadcast([n, 8, 64]), ALU.mult)
                P.tt(t1[:n, :], t1[:n, :], t2[:n, :], ALU.add)
                P.tt(t1[:n, :], t1[:n, :], szg[:n, ti, :], ALU.mult)
                P.act(junk[:n, :512], t1[:n, :], AF.Square, accum_out=ssq[:n, ti, g:g + 1])
                P.copy(yz[:n, :], t1[:n, :], eng="act")
                pty = bf(bank()).rearrange("p (c n) -> p c n", c=8)
                for j in range(4):
                    P.tr(pty[:, j, :n], yz[:n, j * 128:(j + 1) * 128], identb[:n, :n])
                P.tt(ynT[:, 4 * g:4 * g + 4, cs], pty[:, 0:4, :n],
                     ssmn[:, 4 * g:4 * g + 4].unsqueeze(2).to_broadcast([128, 4, n]), ALU.mult)
                if kind == "p":
                    pdS = bank()
                    P.mm(pdS[:, :], Btok[:n, :], xgw[:n, :])
                    hv = hT[:, l, 512 * g:512 * (g + 1)]
                    P.tt(t2[:, :].rearrange("p (h q) -> p h q", h=8), hv.rearrange("p (h q) -> p h q", h=8),
                         decall[:, ti, gh].unsqueeze(2).to_broadcast([128, 8, 64]), ALU.mult)
                    P.tt(hv, t2[:, :], pdS[:, :], ALU.add)
                    P.copy(hTbf[:, 512 * g:512 * (g + 1)], hv, eng="act")

        for (kind, n, c0, ti) in tiles:
            P.reduce_sum(st1[:n, 0:1], ssq[:n, ti, :])
            P.ts(st2[:n, 0:1], st1[:n, 0:1], 1.0 / 2048.0, ALU.mult, EPS, ALU.add)
            P.act(st2[:n, 0:1], st2[:n, 0:1], AF.Ln); P.act(st2[:n, 0:1], st2[:n, 0:1], AF.Exp, scale=-0.5)
            P.copy(hl[:n, 0:1], st2[:n, 0:1])
            P.copy(hlf[:n, 0:1], hl[:n, 0:1])
            P.tt(hl[:n, 1:2], st2[:n, 0:1], hlf[:n, 0:1], ALU.subtract)
            pr_ = bf(bank())
            P.tr(pr_[0:1, 0:n], hl[:n, 0:1], identb[:n, :n])
            P.tr(pr_[0:1, 128:128 + n], hl[:n, 1:2], identb[:n, :n])
            P.copy(rrow[0:1, 0:256], pr_[0:1, 0:256])
            pb = bank()
            P.mm(pb[:, :n], onesb[0:1, :128], rrow[0:1, 0:n], start=True, stop=False)
            P.mm(pb[:, :n], onesb[0:1, :128], rrow[0:1, 128:128 + n], start=False, stop=True)
            P.copy(rstdbc[:, c0:c0 + n], pb[:, :n], eng="act")

        if last:
            hn = tmp("hn", [128, 4, 128], F32)
            hlo = tmp("hlo", [128, 512], BF16)
            hhf = tmp("hhf", [128, 512], F32)
            for q4 in range(4):
                qs = slice(512 * q4, 512 * (q4 + 1))
                P.copy(hhf[:, :], hTbf[:, qs])
                P.tt(hlo[:, :], hT[:, l, qs], hhf[:, :], ALU.subtract)
                pT = bf(bank()).rearrange("p (j n) -> p j n", j=8)
                for j in range(4):
                    jj = 4 * q4 + j
                    P.tr(pT[:, j, :], hTbf[:, jj * 128:(jj + 1) * 128], identb[:, :])
                    P.tr(pT[:, 4 + j, :], hlo[:, j * 128:(j + 1) * 128], identb[:, :])
                P.copy(hn[:, :, :], pT[:, 0:4, :])
                P.tt(hn[:, :, :], hn[:, :, :], pT[:, 4:8, :], ALU.add)
                P.dma("sp", spo[l].rearrange("h q d -> (h q) d").rearrange("(j p) d -> p j d", p=128)[:, 4 * q4:4 * q4 + 4, :],
                      hn[:, :, :])

        if stage_limit < 3:
            return
        ar["o"] = G0
        sgaT = tmp("sgaT", [128, 8, T], BF16)
        sgbT = tmp("sgbT", [128, 8, T], BF16)
        mT = tmp("mT", [128, 8, T], BF16)
        tA = tmp("tA", [128, 2, 512], F32)
        for u in range(2):
            (wga,) = wnext("ga")
            proj_fm(wga, 4, lambda c, c0, c1, ps, u=u: P.act(sgaT[:, 4 * u + c, c0:c1], ps, AF.Sigmoid))
        for u in range(2):
            (wgb,) = wnext("gb")
            proj_fm(wgb, 4, lambda c, c0, c1, ps, u=u: P.act(sgbT[:, 4 * u + c, c0:c1], ps, AF.Sigmoid))
        for u in range(2):
            (wpg,) = wnext("pg")
            for c in range(4):
                dc = 4 * u + c
                for (c0, c1) in blocks:
                    n = c1 - c0
                    pb = bank()
                    for kc in range(8):
                        P.mm(pb[:, :n], wpg[:, kc, c * 128:(c + 1) * 128], ogT[:, kc, c0:c1],
                             start=(kc == 0), stop=(kc == 7))
                    P.tt(mT[:, dc, c0:c1], pb[:, :n], sgaT[:, dc, c0:c1], ALU.mult)
        ka = 0
        for u in range(4):
            (wps,) = wnext("ps")
            for c in range(2):
                dc = 2 * u + c
                for (c0, c1) in blocks:
                    n = c1 - c0
                    pb = bank()
                    for kc in range(16):
                        P.mm(pb[:, :n], wps[:, kc, c * 128:(c + 1) * 128], ynT[:, kc, c0:c1],
                             start=(kc == 0), stop=(kc == 15))
                    ta = tA[:, ka % 2, :n]
                    ka += 1
                    P.tt(ta, pb[:, :n], rstdbc[:, c0:c1], ALU.mult)
                    P.tt(ta, ta, sgbT[:, dc, c0:c1], ALU.mult)
                    P.tt(mT[:, dc, c0:c1], mT[:, dc, c0:c1], ta, ALU.add)
        for u in range(2):
            (wo,) = wnext("wo")
            for (kind, n, c0, ti) in tiles:
                pb = bank()
                for kc in range(8):
                    P.mm(pb[:n, :], mT[:, kc, c0:c0 + n], wo[:, kc, :], start=(kc == 0), stop=(kc == 7))
                xv = x[:n, ti, 512 * u:512 * (u + 1)]
                P.tt(xv, xv, pb[:n, :], ALU.add)

    for pi in range(last_pass):
        tiles = [("p", 128, 128 * i, i) for i in range(4)]
        blocks = [(0, TP)]
        if pi == 0:
            tiles.append(("s", TS, TP, 4))
            blocks.append((TP, T))
        for i in range(4):
            P.dma("sp", x[:, i, :], xp[pi * TP + 128 * i: pi * TP + 128 * (i + 1), :])
        if pi == 0:
            P.dma("sp", x[:TS, 4, :], xs_in[:, :])
        for l in range(NL):
            ffn(tiles, blocks, l, 0)
            if do_mixer:
                mixer(tiles, blocks, l, pi)
            ffn(tiles, blocks, l, 1)
        yo = P.sb("yo", [128, 2, D], F32, at=A0)
        nfin = P.sb("nfin", [128, D], F32, at=A0 + 2 * D * 4)
        P.dma("sp", nfin[:, :], nfin_in[:, :])
        for i, (kind, n, c0, ti) in enumerate(tiles):
            xt = x[:n, ti, :]
            P.act(junk[:n, :], xt, AF.Square, accum_out=st1[:n, 0:1])
            P.ts(st2[:n, 0:1], st1[:n, 0:1], 1.0 / D, ALU.mult, EPS, ALU.add)
            P.act(st2[:n, 0:1], st2[:n, 0:1], AF.Ln); P.act(st2[:n, 0:1], st2[:n, 0:1], AF.Exp, scale=-0.5)
            yv = yo[:n, i % 2, :]
            P.stt(yv, xt, st2[:n, 0:1], nfin[:n, :], ALU.mult, ALU.mult)
            if kind == "p":
                P.dma("sp", yp[pi * TP + c0: pi * TP + c0 + n, :], yv)
            else:
                P.dma("sp", ys[:, :], yv)

    P.emit()
    return nc, P


_CACHE = {}


def kernel(**inputs):
    inp = {k: np.ascontiguousarray(np.asarray(v, dtype=np.float32)) for k, v in inputs.items()}
    if "nc" not in _CACHE:
        _CACHE["nc"] = build()[0]
    nc = _CACHE["nc"]
    consts = make_consts()
    params = make_params(inp)
    nfin_b = np.ascontiguousarray(np.broadcast_to(inp["norm_final"].reshape(1, D), (128, D)))
    bm_b = np.ascontiguousarray(consts[:, C_BM:C_BM + 1024])
    shared = {k: inp[k] for k in ("w_ffn1_gu", "w_ffn2_gu", "w_ffn1_down", "w_ffn2_down", "w_in",
                                  "w_proj_gla", "w_proj_ssm", "w_out")}
    in_maps = []
    for c in range(8):
        m = dict(shared)
        m["consts"] = consts
        m["params"] = params
        m["nfin"] = nfin_b
        m["bm"] = bm_b
        m["xp"] = inp["x_prompt"][c]
        m["xs"] = np.ascontiguousarray(inp["x_sample"][16 * c:16 * (c + 1)].reshape(TS, D))
        m["sg"] = np.ascontiguousarray(inp["state_gla"][:, 16 * c:16 * (c + 1)])
        m["ss"] = np.ascontiguousarray(inp["state_ssm"][:, 16 * c:16 * (c + 1)])
        m["sc"] = np.ascontiguousarray(inp["state_conv"][:, 16 * c:16 * (c + 1)])
        in_maps.append(m)
    res = run_bass_kernel_spmd(nc, in_maps, core_ids=list(range(8)))
    R = res.results
    y_prompt = np.stack([R[c]["yp"] for c in range(8)], 0)
    y_sample = np.concatenate([R[c]["ys"].reshape(16, 4, D) for c in range(8)], 0)
    gla_p = np.stack([R[c]["gp"] for c in range(8)], 1)
    ssm_p = np.stack([R[c]["spo"] for c in range(8)], 1)
    conv_p = np.stack([R[c]["cpo"] for c in range(8)], 1)
    gla_s = np.concatenate([R[c]["gso"] for c in range(8)], 1)
    ssm_s = np.concatenate([R[c]["sso"] for c in range(8)], 1)
    conv_s = np.concatenate([R[c]["cso"] for c in range(8)], 1)
    return (y_prompt.astype(np.float32), y_sample.astype(np.float32), gla_p.astype(np.float32),
            ssm_p.astype(np.float32), conv_p.astype(np.float32), gla_s.astype(np.float32),
            ssm_s.astype(np.float32), conv_s.astype(np.float32))
```
